# Optimizing a Trainium2 kernel written in Bass

```python
import math
import jax, jax.numpy as jnp
from jax import lax
import numpy as np

D_MODEL = 2048
BATCH = 1
SEQ = 8192
DEPTH = 2

CHUNK = 64
Q_BLOCK = 128
EPS = 1e-6

MLA_HEADS = 16
MLA_Q_RANK = 512
MLA_KV_RANK = 512
MLA_NOPE = 128
MLA_ROPE = 64
MLA_V = 128
ROPE_THETA = 10000.0

HG_HEADS = 16
HG_DK = 128
HG_DV = 128
HG_WIDTH = HG_HEADS * HG_DK

SSM_EXPAND = 2
SSM_INNER = SSM_EXPAND * D_MODEL
SSM_HEADDIM = 64
SSM_HEADS = SSM_INNER // SSM_HEADDIM
SSM_GROUPS = 8
SSM_STATE = 128
SSM_CONV = 4
SSM_CONV_DIM = SSM_INNER + 2 * SSM_GROUPS * SSM_STATE

D_FF = 5632
N_BRANCH = 3

IN_SIZES = (MLA_Q_RANK, MLA_KV_RANK, MLA_ROPE,
            HG_WIDTH, HG_WIDTH, HG_WIDTH, HG_WIDTH,
            SSM_INNER, SSM_CONV_DIM, SSM_HEADS,
            N_BRANCH * D_MODEL)
IN_DIM = int(sum(IN_SIZES))
IN_SPLITS = tuple(int(v) for v in np.cumsum(IN_SIZES)[:-1])

kernel_name = "hybrid_mla_hgrn2_mamba2_macaron"


def rmsnorm(x, w):
    xf = x.astype(jnp.float32)
    y = xf * lax.rsqrt(jnp.mean(xf * xf, axis=-1, keepdims=True) + EPS)
    return (y * w.astype(jnp.float32)).astype(x.dtype)


def swiglu(x, w_gate_up, w_down):
    g, u = jnp.split(x @ w_gate_up, 2, axis=-1)
    return (jax.nn.silu(g) * u) @ w_down


def rope_tables(seq, dim):
    inv = 1.0 / (ROPE_THETA ** (jnp.arange(0, dim, 2, dtype=jnp.float32) / dim))
    ang = jnp.arange(seq, dtype=jnp.float32)[:, None] * inv[None, :]
    return jnp.cos(ang), jnp.sin(ang)


def apply_rope(x, cos, sin):
    x1, x2 = jnp.split(x.astype(jnp.float32), 2, axis=-1)
    return jnp.concatenate([x1 * cos - x2 * sin, x1 * sin + x2 * cos], axis=-1).astype(x.dtype)


def tril_mask():
    return jnp.tril(jnp.ones((CHUNK, CHUNK), dtype=bool))


def mla_branch(q_lat, kv_lat, k_pe, q_norm_w, w_uq, kv_norm_w, w_ukv, cos, sin):
    B, S, _ = q_lat.shape
    q = (rmsnorm(q_lat, q_norm_w) @ w_uq).reshape(B, S, MLA_HEADS, MLA_NOPE + MLA_ROPE)
    q_nope = q[..., :MLA_NOPE]
    q_pe = apply_rope(q[..., MLA_NOPE:], cos[:, None, :], sin[:, None, :])
    kv = (rmsnorm(kv_lat, kv_norm_w) @ w_ukv).reshape(B, S, MLA_HEADS, MLA_NOPE + MLA_V)
    k_nope, v = kv[..., :MLA_NOPE], kv[..., MLA_NOPE:]
    k_rot = apply_rope(k_pe, cos, sin)
    scale = (MLA_NOPE + MLA_ROPE) ** -0.5
    n_blk = S // Q_BLOCK
    qn_b = q_nope.reshape(B, n_blk, Q_BLOCK, MLA_HEADS, MLA_NOPE).transpose(1, 0, 2, 3, 4)
    qp_b = q_pe.reshape(B, n_blk, Q_BLOCK, MLA_HEADS, MLA_ROPE).transpose(1, 0, 2, 3, 4)
    key_chunk = jnp.arange(S) // CHUNK

    def block(args):
        i, qn, qp = args
        s = (jnp.einsum('bqhd,bkhd->bhqk', qn, k_nope, preferred_element_type=jnp.float32)
             + jnp.einsum('bqhd,bkd->bhqk', qp, k_rot, preferred_element_type=jnp.float32)) * scale
        q_chunk = (i * Q_BLOCK + jnp.arange(Q_BLOCK)) // CHUNK
        mask = key_chunk[None, :] <= q_chunk[:, None]
        p = jax.nn.softmax(jnp.where(mask, s, -jnp.inf), axis=-1).astype(v.dtype)
        return jnp.einsum('bhqk,bkhd->bqhd', p, v)

    o = lax.map(block, (jnp.arange(n_blk), qn_b, qp_b))
    return o.transpose(1, 0, 2, 3, 4).reshape(B, S, MLA_HEADS * MLA_V)


def hgrn2_branch(q_in, f_in, i_in, g_in, lb, norm_w):
    B, S, _ = q_in.shape
    nc = S // CHUNK
    f32 = jnp.float32
    tril = tril_mask()

    def chunks(t, d):
        return t.astype(f32).reshape(B, nc, CHUNK, HG_HEADS, d).transpose(1, 0, 3, 2, 4)

    f = lb + (1.0 - lb) * jax.nn.sigmoid(f_in.astype(f32))
    q = chunks(jax.nn.silu(q_in.astype(f32)) * HG_DK ** -0.5, HG_DK)
    k = chunks(1.0 - f, HG_DK)
    v = chunks(i_in, HG_DV)
    b = jnp.cumsum(chunks(jnp.log(f), HG_DK), axis=3)

    def step(state, inp):
        qc, kc, vc, bc = inp
        b_last = bc[:, :, -1:, :]
        o_inter = jnp.einsum('bhtk,bhkv->bhtv', qc * jnp.exp(bc), state)
        diff = jnp.where(tril[:, :, None], bc[:, :, :, None, :] - bc[:, :, None, :, :], -jnp.inf)
        att = jnp.einsum('bhtk,bhsk,bhtsk->bhts', qc, kc, jnp.exp(diff))
        o = o_inter + jnp.einsum('bhts,bhsv->bhtv', att, vc)
        state = (jnp.exp(b_last[:, :, 0, :, None]) * state
                 + jnp.einsum('bhsk,bhsv->bhkv', kc * jnp.exp(b_last - bc), vc))
        return state, o

    s0 = jnp.zeros((B, HG_HEADS, HG_DK, HG_DV), f32)
    _, o = lax.scan(step, s0, (q, k, v, b))
    o = o.transpose(1, 0, 3, 2, 4).reshape(B, S, HG_HEADS, HG_DV)
    o = o * lax.rsqrt(jnp.mean(o * o, axis=-1, keepdims=True) + EPS)
    o = o.reshape(B, S, HG_WIDTH) * norm_w.astype(f32) * jax.nn.silu(g_in.astype(f32))
    return o.astype(q_in.dtype)


def mamba2_branch(z, xbc, dt_raw, conv_w, conv_b, a_log, dt_bias, d_skip, norm_w):
    B, S, _ = xbc.shape
    nc = S // CHUNK
    hg = SSM_HEADS // SSM_GROUPS
    f32 = jnp.float32
    tril = tril_mask()
    xbc = lax.conv_general_dilated(xbc, conv_w[:, None, :].astype(xbc.dtype), window_strides=(1,),
                                   padding=[(SSM_CONV - 1, 0)], dimension_numbers=('NWC', 'WIO', 'NWC'),
                                   feature_group_count=SSM_CONV_DIM) + conv_b
    xbc = jax.nn.silu(xbc)
    xs, bm, cm = jnp.split(xbc, [SSM_INNER, SSM_INNER + SSM_GROUPS * SSM_STATE], axis=-1)
    xs = xs.astype(f32).reshape(B, nc, CHUNK, SSM_GROUPS, hg, SSM_HEADDIM)
    bm = bm.astype(f32).reshape(B, nc, CHUNK, SSM_GROUPS, SSM_STATE)
    cm = cm.astype(f32).reshape(B, nc, CHUNK, SSM_GROUPS, SSM_STATE)
    dt = jax.nn.softplus(dt_raw.astype(f32) + dt_bias.astype(f32)).reshape(B, nc, CHUNK, SSM_GROUPS, hg)
    a = -jnp.exp(a_log.astype(f32)).reshape(SSM_GROUPS, hg)
    a_cum = jnp.cumsum((dt * a).transpose(0, 3, 4, 1, 2), axis=-1)
    xdt = xs * dt[..., None]
    seg = a_cum[..., :, None] - a_cum[..., None, :]
    decay = jnp.exp(jnp.where(tril, seg, -jnp.inf))
    cb = jnp.einsum('bclgn,bcsgn->bgcls', cm, bm)
    y_diag = jnp.einsum('bgcls,bghcls,bcsghp->bclghp', cb, decay, xdt)
    decay_states = jnp.exp(a_cum[..., -1:] - a_cum)
    states = jnp.einsum('bcsgn,bghcs,bcsghp->cbghpn', bm, decay_states, xdt)
    chunk_decay = jnp.exp(a_cum[..., -1]).transpose(3, 0, 1, 2)

    def step(h, inp):
        st, dec = inp
        return dec[..., None, None] * h + st, h

    _, h_prev = lax.scan(step, jnp.zeros(states.shape[1:], f32), (states, chunk_decay))
    y_off = jnp.einsum('bclgn,cbghpn,bghcl->bclghp', cm, h_prev, jnp.exp(a_cum))
    y = y_diag + y_off + xs * d_skip.astype(f32).reshape(SSM_GROUPS, hg)[:, :, None]
    y = y.reshape(B, S, SSM_INNER) * jax.nn.silu(z.astype(f32))
    y = y.reshape(B, S, SSM_GROUPS, SSM_INNER // SSM_GROUPS)
    y = y * lax.rsqrt(jnp.mean(y * y, axis=-1, keepdims=True) + EPS)
    y = y.reshape(B, S, SSM_INNER) * norm_w.astype(f32)
    return y.astype(z.dtype)


def setup_inputs(seed: int = 0) -> dict:
    key = jax.random.key(seed)
    ks = iter(jax.random.split(key, 32))
    L = DEPTH

    def nrm(shape, fan_in):
        return jax.random.normal(next(ks), shape, jnp.float32) * fan_in ** -0.5

    def gain(shape):
        return 1.0 + 0.01 * jax.random.normal(next(ks), shape, jnp.float32)

    x = jax.random.normal(next(ks), (BATCH, SEQ, D_MODEL), jnp.float32)
    ffn1_norm = gain((L, D_MODEL))
    ffn1_wi = nrm((L, D_MODEL, 2 * D_FF), D_MODEL)
    ffn1_wo = nrm((L, D_FF, D_MODEL), D_FF)
    mix_norm = gain((L, D_MODEL))
    w_in = nrm((L, D_MODEL, IN_DIM), D_MODEL)
    mla_q_norm = gain((L, MLA_Q_RANK))
    mla_w_uq = nrm((L, MLA_Q_RANK, MLA_HEADS * (MLA_NOPE + MLA_ROPE)), MLA_Q_RANK)
    mla_kv_norm = gain((L, MLA_KV_RANK))
    mla_w_ukv = nrm((L, MLA_KV_RANK, MLA_HEADS * (MLA_NOPE + MLA_V)), MLA_KV_RANK)
    hgrn_lb_logits = 0.5 * jax.random.normal(next(ks), (L, HG_WIDTH), jnp.float32)
    hgrn_norm = gain((L, HG_WIDTH))
    ssm_conv_w = nrm((L, SSM_CONV, SSM_CONV_DIM), SSM_CONV)
    ssm_conv_b = 0.01 * jax.random.normal(next(ks), (L, SSM_CONV_DIM), jnp.float32)
    ssm_a_log = jnp.log(jax.random.uniform(next(ks), (L, SSM_HEADS), jnp.float32, 1.0, 16.0))
    dt0 = jnp.exp(jax.random.uniform(next(ks), (L, SSM_HEADS), jnp.float32, math.log(1e-3), math.log(1e-1)))
    ssm_dt_bias = dt0 + jnp.log(-jnp.expm1(-dt0))
    ssm_d = gain((L, SSM_HEADS))
    ssm_norm = gain((L, SSM_INNER))
    w_o_mla = nrm((L, MLA_HEADS * MLA_V, D_MODEL), MLA_HEADS * MLA_V)
    w_o_hgrn = nrm((L, HG_WIDTH, D_MODEL), HG_WIDTH)
    w_o_ssm = nrm((L, SSM_INNER, D_MODEL), SSM_INNER)
    w_out = nrm((L, D_MODEL, D_MODEL), D_MODEL)
    ffn2_norm = gain((L, D_MODEL))
    ffn2_wi = nrm((L, D_MODEL, 2 * D_FF), D_MODEL)
    ffn2_wo = nrm((L, D_FF, D_MODEL), D_FF)
    final_norm = gain((D_MODEL,))
    return {"x": x, "ffn1_norm": ffn1_norm, "ffn1_wi": ffn1_wi, "ffn1_wo": ffn1_wo,
            "mix_norm": mix_norm, "w_in": w_in, "mla_q_norm": mla_q_norm, "mla_w_uq": mla_w_uq,
            "mla_kv_norm": mla_kv_norm, "mla_w_ukv": mla_w_ukv, "hgrn_lb_logits": hgrn_lb_logits,
            "hgrn_norm": hgrn_norm, "ssm_conv_w": ssm_conv_w, "ssm_conv_b": ssm_conv_b,
            "ssm_a_log": ssm_a_log, "ssm_dt_bias": ssm_dt_bias, "ssm_d": ssm_d, "ssm_norm": ssm_norm,
            "w_o_mla": w_o_mla, "w_o_hgrn": w_o_hgrn, "w_o_ssm": w_o_ssm, "w_out": w_out,
            "ffn2_norm": ffn2_norm, "ffn2_wi": ffn2_wi, "ffn2_wo": ffn2_wo, "final_norm": final_norm}


def reference(x, ffn1_norm, ffn1_wi, ffn1_wo, mix_norm, w_in, mla_q_norm, mla_w_uq, mla_kv_norm,
              mla_w_ukv, hgrn_lb_logits, hgrn_norm, ssm_conv_w, ssm_conv_b, ssm_a_log, ssm_dt_bias,
              ssm_d, ssm_norm, w_o_mla, w_o_hgrn, w_o_ssm, w_out, ffn2_norm, ffn2_wi, ffn2_wo,
              final_norm):
    S = x.shape[1]
    cos, sin = rope_tables(S, MLA_ROPE)
    p = jax.nn.softmax(hgrn_lb_logits.astype(jnp.float32), axis=0)
    lower_bounds = jnp.cumsum(p, axis=0) - p[0:1]
    for l in range(DEPTH):
        x = x + 0.5 * swiglu(rmsnorm(x, ffn1_norm[l]), ffn1_wi[l], ffn1_wo[l])
        h = rmsnorm(x, mix_norm[l])
        (q_lat, kv_lat, k_pe, hq, hf, hi, hgate, z, xbc, dt_raw, gates) = jnp.split(h @ w_in[l], IN_SPLITS, axis=-1)
        y_a = mla_branch(q_lat, kv_lat, k_pe, mla_q_norm[l], mla_w_uq[l], mla_kv_norm[l], mla_w_ukv[l], cos, sin) @ w_o_mla[l]
        y_b = hgrn2_branch(hq, hf, hi, hgate, lower_bounds[l], hgrn_norm[l]) @ w_o_hgrn[l]
        y_c = mamba2_branch(z, xbc, dt_raw, ssm_conv_w[l], ssm_conv_b[l], ssm_a_log[l], ssm_dt_bias[l],
                            ssm_d[l], ssm_norm[l]) @ w_o_ssm[l]
        g_a, g_b, g_c = jnp.split(jax.nn.sigmoid(gates), N_BRANCH, axis=-1)
        x = x + (g_a * y_a + g_b * y_b + g_c * y_c) @ w_out[l]
        x = x + 0.5 * swiglu(rmsnorm(x, ffn2_norm[l]), ffn2_wi[l], ffn2_wo[l])
    return rmsnorm(x, final_norm)
```

```python
import contextlib
import numpy as np
import ml_dtypes
import concourse.bass as bass
import concourse.mybir as mybir
from concourse.bass_utils import run_bass_kernel_spmd

F32 = mybir.dt.float32
BF16 = mybir.dt.bfloat16
AF = mybir.ActivationFunctionType
ALU = mybir.AluOpType
AX = mybir.AxisListType

SAME_ENG_SYNC = "auto"
NDMA_SEMS = 36
NCORES = 8

D = 2048
DFF = 5632
KC = D // 128
FC = DFF // 128
EPS = 1e-6
SEQ = 8192

O_QLAT, O_KVLAT, O_KPE = 0, 512, 1024
O_HQ, O_HF, O_HI, O_HG = 1088, 1088 + 2048, 1088 + 4096, 1088 + 6144
O_Z = 1088 + 8192
O_XBC = O_Z + 4096
O_DT = O_XBC + 6144
O_GATE = O_DT + 64
IN_DIM = O_GATE + 3 * D


class Buf:
    __slots__ = ("name", "w", "r", "excl")

    def __init__(self, name, excl=False):
        self.name = name
        self.w = None
        self.r = []
        self.excl = excl


class V:
    __slots__ = ("ap", "bufs")

    def __init__(self, ap, bufs):
        self.ap = ap
        self.bufs = bufs

    def __getitem__(self, k):
        return V(self.ap[k], self.bufs)

    def sub(self, name, k=None):
        ap = self.ap if k is None else self.ap[k]
        return V(ap, [Buf(name)])

    def wap(self, ap):
        return V(ap, self.bufs)

    def re(self, s, **kw):
        return V(self.ap.rearrange(s, **kw), self.bufs)

    def bc(self, shape):
        return V(self.ap.broadcast_to(list(shape)), self.bufs)


class Op:
    __slots__ = ("eng", "fn", "deps", "dma", "slot", "same")

    def __init__(self, eng, fn, deps, dma, slot, same=True):
        self.same = same
        self.eng = eng
        self.fn = fn
        self.deps = deps
        self.dma = dma
        self.slot = slot


ENGS = ("pe", "act", "dve", "pool", "sp")


class Ctx:
    ARENA_BYTES = 204 * 1024

    def __init__(self):
        self.nc = bass.Bass("TRN2", target_bir_lowering=False)
        self.es = contextlib.ExitStack()
        self.ops = []
        self.rr = 0
        self.rrq = [0, 0, 0]
        self.same = True
        self.coll_ops = []
        self.slot_last = [None] * NDMA_SEMS
        self.last_op = {}
        self.fence_deps = {}
        self.arena = self.nc.alloc_sbuf_tensor("arena", [128, self.ARENA_BYTES // 4], F32)[:]
        self.bump = 0
        self.allocs = []
        self.marks = []
        self.banks = []
        for i in range(8):
            t = self.nc.alloc_psum_tensor("bank%d" % i, [128, 512], F32)
            self.banks.append(V(t[:], [Buf("bank%d" % i, excl=True)]))

    def dram(self, name, shape, dtype, kind="Internal"):
        t = self.nc.dram_tensor(name, list(shape), dtype, kind=kind)
        return V(t.ap(), [Buf(name)])

    def sbuf(self, name, shape, dtype):
        esz = 2 if dtype == BF16 else 4
        n = 1
        for s in shape[1:]:
            n *= s
        nbytes = (n * esz + 31) // 32 * 32
        assert self.bump + nbytes <= self.ARENA_BYTES, ("SBUF overflow", name, self.bump, nbytes)
        ap = self.arena[0:shape[0], self.bump // 4:(self.bump + nbytes) // 4]
        lo, hi = self.bump, self.bump + nbytes
        nb = Buf(name)
        olds = [b for (s0, e0, b) in self.allocs if s0 < hi and lo < e0]
        self.allocs.append((lo, hi, nb))
        self.bump += nbytes
        if dtype == BF16:
            ap = ap.bitcast(BF16)
        ap = ap[:, 0:n]
        if len(shape) == 3:
            ap = ap.rearrange("p (a b) -> p a b", a=shape[1])
        elif len(shape) == 4:
            ap = ap.rearrange("p (a b c) -> p a b c", a=shape[1], b=shape[2])
        return V(ap, [nb])

    def mark(self):
        self.marks.append(self.bump)

    def release(self):
        self.bump = self.marks.pop()
        self.fence()

    def bank(self, i, dtype=F32, shape=None, name=None):
        b = self.banks[i]
        ap = b.ap
        if dtype == BF16:
            ap = ap.bitcast(BF16)
        if shape is not None:
            n = 1
            for s in shape[1:]:
                n *= s
            ap = ap[0:shape[0], 0:n]
            if len(shape) == 3:
                ap = ap.rearrange("p (a b) -> p a b", a=shape[1])
        return V(ap, b.bufs if name is None else [Buf(name)])

    def fence(self):
        F = set(self.last_op.values())
        F.update(j for j in self.slot_last if j is not None)
        F.update(self.coll_ops)
        for e in ENGS:
            self.fence_deps.setdefault(e, set()).update(F)

    def rec(self, eng, fn, reads, writes, dma=False, coll=False):
        idx = len(self.ops)
        deps = set()
        fd = self.fence_deps.pop(eng, None)
        if fd:
            deps.update(fd)
        rb = [b for v in reads for b in v.bufs if not b.excl]
        wb = [b for v in writes for b in v.bufs] + [b for v in reads for b in v.bufs if b.excl]
        for b in rb:
            if b.w is not None:
                deps.add(b.w)
        for b in wb:
            if b.w is not None:
                deps.add(b.w)
            deps.update(b.r)
        for b in wb:
            b.w = idx
            b.r = []
        for b in rb:
            if True:
                if not dma:
                    b.r = [j for j in b.r if j != idx and (self.ops[j].dma or self.ops[j].eng != eng)]
                b.r.append(idx)
        slot = None
        if coll:
            slot = -1
            self.coll_ops.append(idx)
        elif dma:
            qi = {"pool": 0, "sp": 1, "act": 2}[eng]
            per = NDMA_SEMS // 3
            slot = qi * per + self.rrq[qi] % per
            self.rrq[qi] += 1
            if self.slot_last[slot] is not None:
                deps.add(self.slot_last[slot])
            self.slot_last[slot] = idx
        else:
            self.last_op[eng] = idx
        deps.discard(idx)
        self.ops.append(Op(eng, fn, deps, dma, slot, self.same))
        return idx

    def dma(self, out, in_, q="sp", slow=False):
        e = {"sp": self.nc.sync, "pool": self.nc.gpsimd, "act": self.nc.scalar}[q]
        if slow:
            fn = lambda: e.dma_start(out=out.ap, in_=in_.ap, allow_slow_non_contiguous=True)
        else:
            fn = lambda: e.dma_start(out=out.ap, in_=in_.ap)
        self.rec(q, fn, [in_], [out], dma=True)

    def mm(self, out, lhsT, rhs, start=True, stop=True):
        nc = self.nc
        self.rec("pe", lambda: nc.tensor.matmul(out.ap, lhsT.ap, rhs.ap, start=start, stop=stop),
                 [lhsT, rhs] + ([] if start else [out]), [out])

    def transpose(self, out, in_, ident):
        nc = self.nc
        self.rec("pe", lambda: nc.tensor.transpose(out.ap, in_.ap, ident.ap), [in_, ident], [out])

    def act(self, out, in_, func, bias=None, scale=None, accum_out=None):
        nc = self.nc
        reads = [in_]
        kw = {}
        if bias is not None:
            if isinstance(bias, V):
                reads.append(bias)
                kw["bias"] = bias.ap
            else:
                kw["bias"] = float(bias)
        if scale is not None:
            if isinstance(scale, V):
                reads.append(scale)
                kw["scale"] = scale.ap
            else:
                kw["scale"] = float(scale)
        writes = [out]
        if accum_out is not None:
            kw["accum_out"] = accum_out.ap
            writes.append(accum_out)
        self.rec("act", lambda: nc.scalar.activation(out=out.ap, in_=in_.ap, func=func, **kw), reads, writes)

    def _ve(self, eng):
        return {"dve": self.nc.vector, "pool": self.nc.gpsimd}[eng]

    def tt(self, out, in0, in1, op, eng="dve"):
        e = self._ve(eng)
        self.rec(eng, lambda: e.tensor_tensor(out=out.ap, in0=in0.ap, in1=in1.ap, op=op), [in0, in1], [out])

    def ts(self, out, in0, s1, op0, s2=None, op1=None, eng="dve", accum_out=None):
        e = self._ve(eng)
        reads = [in0]
        a1 = s1
        a2 = s2
        if isinstance(s1, V):
            reads.append(s1)
            a1 = s1.ap
        if isinstance(s2, V):
            reads.append(s2)
            a2 = s2.ap
        writes = [out]
        kw = {}
        if op1 is not None:
            kw["op1"] = op1
        if accum_out is not None:
            kw["accum_out"] = accum_out.ap
            writes.append(accum_out)
        self.rec(eng, lambda: e.tensor_scalar(out=out.ap, in0=in0.ap, scalar1=a1, scalar2=a2, op0=op0, **kw),
                 reads, writes)

    def stt(self, out, in0, scalar, in1, op0, op1):
        nc = self.nc
        reads = [in0, in1]
        a = scalar
        if isinstance(scalar, V):
            reads.append(scalar)
            a = scalar.ap
        self.rec("dve", lambda: nc.vector.scalar_tensor_tensor(out=out.ap, in0=in0.ap, scalar=a, in1=in1.ap,
                                                                op0=op0, op1=op1), reads, [out])

    def copy(self, out, in_, eng="dve"):
        if eng == "act":
            nc = self.nc
            self.rec("act", lambda: nc.scalar.copy(out=out.ap, in_=in_.ap), [in_], [out])
        else:
            e = self._ve(eng)
            self.rec(eng, lambda: e.tensor_copy(out=out.ap, in_=in_.ap), [in_], [out])

    def memset(self, out, val, eng="dve"):
        e = self._ve(eng)
        self.rec(eng, lambda: e.memset(out.ap, val), [], [out])

    def recip(self, out, in_):
        nc = self.nc
        self.rec("dve", lambda: nc.vector.reciprocal(out=out.ap, in_=in_.ap), [in_], [out])

    def reduce(self, out, in_, op=ALU.add, axis=AX.X):
        nc = self.nc
        self.rec("dve", lambda: nc.vector.tensor_reduce(out=out.ap, in_=in_.ap, axis=axis, op=op), [in_], [out])

    def finish(self):
        self.fence()
        self.rec("sp", lambda: None, [], [])

    def emit(self):
        nc = self.nc
        ops = self.ops
        n = len(ops)
        engs = {"pe": nc.tensor, "act": nc.scalar, "dve": nc.vector, "pool": nc.gpsimd, "sp": nc.sync}
        esem = {k: self.es.enter_context(nc.semaphore("s_" + k)) for k in engs}
        dsem = [self.es.enter_context(nc.semaphore("d%d" % i)) for i in range(NDMA_SEMS)]
        dval = [0] * NDMA_SEMS

        def needs(d, i):
            a, b = ops[d], ops[i]
            if a.dma:
                return True
            if a.eng == "sp":
                return False
            if a.eng == b.eng and not b.dma:
                if a.eng == "pe":
                    return False
                if b.same == "auto":
                    return a.eng in ("act", "pool")
                return b.same
            return True

        sig = [False] * n
        for i, op in enumerate(ops):
            for d in op.deps:
                if needs(d, i):
                    sig[d] = True
        cnt = {k: 0 for k in engs}
        val = [None] * n
        seen = {k: {} for k in engs}
        nwait = 0
        for i, op in enumerate(ops):
            E = op.eng
            waits = {}
            for d in op.deps:
                if needs(d, i):
                    key, s, v = val[d]
                    if key not in waits or waits[key][1] < v:
                        waits[key] = (s, v)
            sd = seen[E]
            for key, (s, v) in waits.items():
                if sd.get(key, 0) < v:
                    engs[E].wait_ge(s, v)
                    sd[key] = v
                    nwait += 1
            ins = op.fn()
            if ins is None:
                continue
            if op.dma and op.slot == -1:
                cs = self.es.enter_context(nc.semaphore("cc%d" % i))
                val[i] = ("cc%d" % i, cs, 1)
                ins.then_inc(cs)
            elif op.dma:
                k = op.slot
                dval[k] += 16
                val[i] = ("d%d" % k, dsem[k], dval[k])
                ins.then_inc(dsem[k], 16)
            elif sig[i]:
                cnt[E] += 1
                val[i] = (E, esem[E], cnt[E])
                ins.then_inc(esem[E], 1)
        self.stats = dict(n_ops=n, n_wait=nwait, n_sig=sum(sig))
        return nc


def load_cols(c, name, vec, n):
    t = c.sbuf(name, [128, n], F32)
    c.dma(t, vec.re("(j p) -> p j", p=128), slow=True)
    return t


def norm_transpose(c, K, x_src, nT, hT, nwc, Dm=D, eps=EPS, src_sbuf=None, tag="nt"):
    kc_n = Dm // 128
    xts = [c.sbuf(tag + "_x%d" % i, [128, Dm], F32) for i in range(2)] if src_sbuf is None else None
    junk = c.sbuf(tag + "_junk", [128, Dm], F32)
    xns = [c.sbuf(tag + "_xn%d" % i, [128, Dm], BF16) for i in range(2)]
    st = [c.sbuf(tag + "_st%d" % i, [128, 4], F32) for i in range(2)]
    for t in range(nT):
        if src_sbuf is None:
            xt = xts[t % 2]
            c.dma(xt, x_src[t * 128:(t + 1) * 128, :])
        else:
            xt = src_sbuf[t]
        s = st[t % 2]
        xn = xns[t % 2]
        c.act(junk, xt, AF.Square)
        c.reduce(s[:, 0:1], junk)
        c.act(s[:, 1:2], s[:, 0:1], AF.Sqrt, scale=1.0 / Dm, bias=K["eps"])
        c.recip(s[:, 2:3], s[:, 1:2])
        c.ts(xn, xt, s[:, 2:3], ALU.mult)
        for g in range((kc_n + 7) // 8):
            ng = min(8, kc_n - g * 8)
            pt = c.bank(K["tb"][(t * 2 + g) % 2], BF16, [128, 8, 128])
            for j in range(ng):
                kc = g * 8 + j
                c.transpose(pt[:, j, :], xn[:, kc * 128:(kc + 1) * 128], K["ident"])
            c.tt(hT[:, g * 8:g * 8 + ng, t * 128:(t + 1) * 128], pt[:, 0:ng, :],
                 nwc[:, g * 8:g * 8 + ng].wap(nwc.ap[:, g * 8:g * 8 + ng].unsqueeze(2).broadcast_to([128, ng, 128])),
                 ALU.mult)


def setup_consts(c, ident_d):
    K = {}
    K["ident"] = c.sbuf("ident", [128, 128], BF16)
    c.dma(K["ident"], ident_d, q="pool")
    K["eps"] = c.sbuf("epsc", [128, 1], F32)
    c.memset(K["eps"], EPS)
    K["tb"] = (6, 7)
    return K


def ffn_block(c, K, x_src, x_dst, nw, wi, wo, TOK):
    nT = TOK // 128
    TG = min(512, TOK)
    nG = TOK // TG
    c.mark()
    aT = c.sbuf("f_aT", [128, FC, TOK], BF16)
    c.mark()
    hT = c.sbuf("f_hT", [128, KC, TOK], BF16)
    nwc = load_cols(c, "f_nwc", nw, KC)
    c.mark()
    norm_transpose(c, K, x_src, nT, hT, nwc, tag="f_nt")
    c.release()
    WB = 256
    NWB = 3
    nblk = DFF // WB
    wg = [c.sbuf("f_wg%d" % i, [128, KC, WB], BF16) for i in range(NWB)]
    wu = [c.sbuf("f_wu%d" % i, [128, KC, WB], BF16) for i in range(NWB)]
    sg = [c.sbuf("f_sg%d" % i, [128, TG], F32) for i in range(2)]
    it = 0
    for b in range(nblk):
        g_t, u_t = wg[b % NWB], wu[b % NWB]
        c.dma(g_t, wi[:, b * WB:(b + 1) * WB].re("(kc p) n -> p kc n", p=128), q="pool")
        c.dma(u_t, wi[:, DFF + b * WB:DFF + (b + 1) * WB].re("(kc p) n -> p kc n", p=128), q="pool")
        for j in range(WB // 128):
            fc = b * (WB // 128) + j
            for tg in range(nG):
                pg = c.bank((it * 2) % 6, F32, [128, TG])
                pu = c.bank((it * 2 + 1) % 6, F32, [128, TG])
                for kc in range(KC):
                    c.mm(pg, g_t[:, kc, j * 128:(j + 1) * 128], hT[:, kc, tg * TG:(tg + 1) * TG],
                         start=(kc == 0), stop=(kc == KC - 1))
                for kc in range(KC):
                    c.mm(pu, u_t[:, kc, j * 128:(j + 1) * 128], hT[:, kc, tg * TG:(tg + 1) * TG],
                         start=(kc == 0), stop=(kc == KC - 1))
                s = sg[it % 2]
                c.act(s, pg, AF.Silu)
                c.tt(aT[:, fc, tg * TG:(tg + 1) * TG], pu, s, ALU.mult)
                it += 1
    c.release()
    c.mark()
    FG = 4
    wob = [c.sbuf("f_wo%d" % i, [128, FG, 512], BF16) for i in range(4)]
    xres = [c.sbuf("f_xr%d" % i, [128, 512], F32) for i in range(16)]
    n_pass_t = (nT + 7) // 8
    wi_ = 0
    pi = 0
    for tp in range(n_pass_t):
        tts = list(range(tp * 8, min(nT, tp * 8 + 8)))
        for dc in range(D // 512):
            xrs = xres[(pi % 2) * 8:(pi % 2) * 8 + 8]
            pi += 1
            for ti, t in enumerate(tts):
                c.dma(xrs[ti], x_src[t * 128:(t + 1) * 128, dc * 512:(dc + 1) * 512])
            for fg in range(FC // FG):
                w_t = wob[wi_ % 4]
                wi_ += 1
                c.dma(w_t, wo[fg * FG * 128:(fg + 1) * FG * 128, dc * 512:(dc + 1) * 512].re("(f p) n -> p f n", p=128),
                      q="pool")
                for ti, t in enumerate(tts):
                    for f in range(FG):
                        fc = fg * FG + f
                        c.mm(c.bank(ti), aT[:, fc, t * 128:(t + 1) * 128], w_t[:, f, :],
                             start=(fc == 0), stop=(fc == FC - 1))
            for ti, t in enumerate(tts):
                xr = xrs[ti]
                c.stt(xr, c.bank(ti), 0.5, xr, ALU.mult, ALU.add)
                c.dma(x_dst[t * 128:(t + 1) * 128, dc * 512:(dc + 1) * 512], xr)
    c.release()
    c.release()


def final_norm_block(c, K, x_src, out, nw, TOK):
    nT = TOK // 128
    c.mark()
    wbc = c.sbuf("fn_w", [128, D], F32)
    c.dma(wbc, nw.wap(nw.ap.partition_broadcast(128)))
    xt = [c.sbuf("fn_x%d" % i, [128, D], F32) for i in range(2)]
    junk = c.sbuf("fn_junk", [128, D], F32)
    st = [c.sbuf("fn_st%d" % i, [128, 4], F32) for i in range(2)]
    for t in range(nT):
        x = xt[t % 2]
        s = st[t % 2]
        c.dma(x, x_src[t * 128:(t + 1) * 128, :])
        c.act(junk, x, AF.Square)
        c.reduce(s[:, 0:1], junk)
        c.act(s[:, 1:2], s[:, 0:1], AF.Sqrt, scale=1.0 / D, bias=K["eps"])
        c.recip(s[:, 2:3], s[:, 1:2])
        c.stt(x, x, s[:, 2:3], wbc, ALU.mult, ALU.mult)
        c.dma(out[t * 128:(t + 1) * 128, :], x)
    c.release()


def latents_block(c, K, x1, hT_out, latT_out, mix_nw, w_in, qn_w, kvn_w, rope_cs, TOK):
    nT = TOK // 128
    c.mark()
    hT = c.sbuf("l_hT", [128, KC, TOK], BF16)
    nwc = load_cols(c, "l_nwc", mix_nw, KC)
    qnc = load_cols(c, "l_qnc", qn_w, 4)
    kvnc = load_cols(c, "l_kvnc", kvn_w, 4)
    wl = c.sbuf("l_wl", [128, KC, 1088], BF16)
    c.dma(wl[:, :, 0:512], w_in[:, 0:512].re("(kc p) n -> p kc n", p=128), q="pool")
    c.dma(wl[:, :, 512:1024], w_in[:, 512:1024].re("(kc p) n -> p kc n", p=128), q="pool")
    c.dma(wl[:, :, 1024:1088], w_in[:, 1024:1088].re("(kc p) n -> p kc n", p=128), q="pool")
    c.mark()
    norm_transpose(c, K, x1, nT, hT, nwc, tag="l_nt")
    c.release()
    c.dma(hT_out.re("(kc p) t -> p kc t", p=128), hT)
    latT = c.sbuf("l_latT", [128, 9, TOK], BF16)
    c.mark()
    cs = [c.sbuf("l_cs%d" % i, [128, 64], F32) for i in range(2)]
    kr = [c.sbuf("l_kr%d" % i, [128, 128], BF16) for i in range(2)]
    tmp = [c.sbuf("l_tmp%d" % i, [128, 4, 32], F32) for i in range(2)]
    for i in range(2):
        c.memset(kr[i], 0.0)
    junk = c.sbuf("l_junk", [128, 512], F32)
    xns = [c.sbuf("l_xn%d" % i, [128, 512], BF16) for i in range(2)]
    sts = [c.sbuf("l_st%d" % i, [128, 4], F32) for i in range(4)]
    for t in range(nT):
        pq = c.bank(0)
        pkv = c.bank(1)
        pk = c.bank(2, F32, [128, 64])
        for kc in range(KC):
            c.mm(pq, hT[:, kc, t * 128:(t + 1) * 128], wl[:, kc, 0:512], start=(kc == 0), stop=(kc == KC - 1))
        for kc in range(KC):
            c.mm(pkv, hT[:, kc, t * 128:(t + 1) * 128], wl[:, kc, 512:1024], start=(kc == 0), stop=(kc == KC - 1))
        for kc in range(KC):
            c.mm(pk, hT[:, kc, t * 128:(t + 1) * 128], wl[:, kc, 1024:1088], start=(kc == 0), stop=(kc == KC - 1))
        for li, (src, wcol) in enumerate(((pq, qnc), (pkv, kvnc))):
            s = sts[(t * 2 + li) % 4]
            xn = xns[li]
            c.act(junk, src, AF.Square)
            c.reduce(s[:, 0:1], junk)
            c.act(s[:, 1:2], s[:, 0:1], AF.Sqrt, scale=1.0 / 512, bias=K["eps"])
            c.recip(s[:, 2:3], s[:, 1:2])
            c.ts(xn, src, s[:, 2:3], ALU.mult)
            pt = c.bank(K["tb"][li], BF16, [128, 8, 128])
            for j in range(4):
                c.transpose(pt[:, j, :], xn[:, j * 128:(j + 1) * 128], K["ident"])
            c.tt(latT[:, li * 4:li * 4 + 4, t * 128:(t + 1) * 128], pt[:, 0:4, :],
                 wcol.wap(wcol.ap.unsqueeze(2).broadcast_to([128, 4, 128])), ALU.mult)
        csb = cs[t % 2]
        c.dma(csb, rope_cs[t * 128:(t + 1) * 128, :])
        tm = tmp[t % 2]
        krt = kr[t % 2]
        c.tt(tm[:, 0, :], pk[:, 0:32], csb[:, 0:32], ALU.mult)
        c.tt(tm[:, 1, :], pk[:, 32:64], csb[:, 32:64], ALU.mult)
        c.tt(tm[:, 2, :], pk[:, 0:32], csb[:, 32:64], ALU.mult)
        c.tt(tm[:, 3, :], pk[:, 32:64], csb[:, 0:32], ALU.mult)
        c.tt(krt[:, 0:32], tm[:, 0, :], tm[:, 1, :], ALU.subtract)
        c.tt(krt[:, 32:64], tm[:, 2, :], tm[:, 3, :], ALU.add)
        pt = c.bank(3, BF16, [128, 128])
        c.transpose(pt, krt, K["ident"])
        c.copy(latT[0:64, 8, t * 128:(t + 1) * 128], pt[0:64, :])
    c.release()
    c.dma(latT_out[0:1024, :].re("(j p) t -> p j t", p=128), latT[:, 0:8, :])
    c.dma(latT_out[1024:1088, :], latT[0:64, 8, :])
    c.release()


def rope_table(S):
    inv = 1.0 / (10000.0 ** (np.arange(0, 64, 2, dtype=np.float32) / np.float32(64)))
    ang = np.arange(S, dtype=np.float32)[:, None] * inv[None, :].astype(np.float32)
    return np.concatenate([np.cos(ang), np.sin(ang)], axis=1).astype(np.float32)


def bf16(a):
    return np.asarray(a).astype(ml_dtypes.bfloat16)


def build_p1(TOK):
    c = Ctx()
    x = c.dram("x", [TOK, D], F32, "ExternalInput")
    nw = c.dram("ffn_nw", [D], F32, "ExternalInput")
    wi = c.dram("ffn_wi", [D, 2 * DFF], F32, "ExternalInput")
    wo = c.dram("ffn_wo", [DFF, D], F32, "ExternalInput")
    mnw = c.dram("mix_nw", [D], F32, "ExternalInput")
    wlat = c.dram("w_lat", [D, 1088], F32, "ExternalInput")
    qn = c.dram("qn_w", [512], F32, "ExternalInput")
    kvn = c.dram("kvn_w", [512], F32, "ExternalInput")
    rope = c.dram("rope_cs", [TOK, 64], F32, "ExternalInput")
    ident = c.dram("ident", [128, 128], F32, "ExternalInput")
    x1 = c.dram("x1", [TOK, D], F32, "ExternalOutput")
    hT = c.dram("hT", [D, TOK], BF16, "ExternalOutput")
    latT = c.dram("latT", [1088, TOK], BF16, "ExternalOutput")
    K = setup_consts(c, ident)
    ffn_block(c, K, x, x1, nw, wi, wo, TOK)
    latents_block(c, K, x1, hT, latT, mnw, wlat, qn, kvn, rope, TOK)
    c.finish()
    c.emit()
    return c


def mla_block(c, K, lat_src, kr_src, wq_d, wkv_d, rope_cs, out_dst, S):
    TQ = min(512, S)
    nQB = S // TQ
    nTT = TQ // 128
    scale = 192.0 ** -0.5
    c.same = "auto"
    c.mark()
    knT = c.sbuf("m_knT", [128, 2, S], BF16)
    Vt = c.sbuf("m_Vt", [128, S // 128, 256], BF16)
    krA = c.sbuf("m_krA", [128, S], BF16)
    krB = c.sbuf("m_krB", [128, S], BF16)
    c.memset(krA[64:128, :], 0.0)
    c.memset(krB[0:64, :], 0.0)
    ones = c.sbuf("m_ones", [128, 128], BF16)
    c.memset(ones, 1.0)
    wq = c.sbuf("m_wq", [128, 4, 384], BF16)
    wkv = c.sbuf("m_wkv", [128, 4, 512], BF16)
    c.dma(wq, wq_d.re("(j p) n -> p j n", p=128), q="pool")
    c.dma(wkv, wkv_d.re("(j p) n -> p j n", p=128), q="pool")
    lat = [c.sbuf("m_lat%d" % i, [128, 8, TQ], BF16) for i in range(2)]
    qnT = [c.sbuf("m_qnT%d" % i, [128, 2, TQ], BF16) for i in range(2)]
    qpT = [c.sbuf("m_qpT%d" % i, [128, TQ], BF16) for i in range(2)]
    cs = [c.sbuf("m_cs%d" % i, [128, 64], F32) for i in range(2)]
    tmp = [c.sbuf("m_tmp%d" % i, [128, 4, 2, 32], F32) for i in range(2)]
    qpe = [c.sbuf("m_qpe%d" % i, [128, 2, 2, 32], BF16) for i in range(2)]
    PT = [c.sbuf("m_PT%d" % i, [128, TQ], BF16) for i in range(4)]
    rl = [c.sbuf("m_rl%d" % i, [128, TQ], F32) for i in range(2)]
    oT = [c.sbuf("m_oT%d" % i, [128, TQ], BF16) for i in range(2)]
    rb = [0]

    def rbank(dtype=F32, shape=None):
        b = c.bank(rb[0] % 4, dtype, shape)
        rb[0] += 1
        return b

    blk = [0]

    def prologue(qb):
        t0 = qb * TQ
        lt = lat[qb % 2]
        c.dma(lt, lat_src(t0, TQ).re("(j p) t -> p j t", p=128))
        c.dma(krA[0:64, t0:t0 + TQ], kr_src(t0, TQ))
        c.dma(krB[64:128, t0:t0 + TQ], kr_src(t0, TQ))
        qn = qnT[qb % 2]
        qp = qpT[qb % 2]
        for h in range(2):
            pk = rbank(F32, [128, TQ])
            for j in range(4):
                c.mm(pk, wkv[:, j, h * 128:(h + 1) * 128], lt[:, 4 + j, :], start=(j == 0), stop=(j == 3))
            c.copy(knT[:, h, t0:t0 + TQ], pk, eng="act")
            pq = rbank(F32, [128, TQ])
            for j in range(4):
                c.mm(pq, wq[:, j, h * 128:(h + 1) * 128], lt[:, j, :], start=(j == 0), stop=(j == 3))
            c.copy(qn[:, h, :], pq)
        for tt in range(nTT):
            gt = (t0 // 128) + tt
            pv = rbank(F32, [128, 256])
            for j in range(4):
                c.mm(pv, lt[:, 4 + j, tt * 128:(tt + 1) * 128], wkv[:, j, 256:512], start=(j == 0), stop=(j == 3))
            c.copy(Vt[:, gt, :], pv, eng="act")
            pr = rbank(F32, [128, 128])
            for j in range(4):
                c.mm(pr, lt[:, j, tt * 128:(tt + 1) * 128], wq[:, j, 256:384], start=(j == 0), stop=(j == 3))
            csb = cs[tt % 2]
            c.dma(csb, rope_cs[t0 + tt * 128:t0 + (tt + 1) * 128, :])
            tm = tmp[tt % 2]
            qe = qpe[tt % 2]
            prv = pr.wap(pr.ap.rearrange("p (h two i) -> p h two i", h=2, two=2))
            cosb = csb.wap(csb.ap[:, 0:32].unsqueeze(1).broadcast_to([128, 2, 32]))
            sinb = csb.wap(csb.ap[:, 32:64].unsqueeze(1).broadcast_to([128, 2, 32]))
            c.tt(tm[:, 0, :, :], prv[:, :, 0, :], cosb, ALU.mult)
            c.tt(tm[:, 1, :, :], prv[:, :, 1, :], sinb, ALU.mult)
            c.tt(tm[:, 2, :, :], prv[:, :, 0, :], sinb, ALU.mult)
            c.tt(tm[:, 3, :, :], prv[:, :, 1, :], cosb, ALU.mult)
            c.tt(qe[:, :, 0, :], tm[:, 0, :, :], tm[:, 1, :, :], ALU.subtract)
            c.tt(qe[:, :, 1, :], tm[:, 2, :, :], tm[:, 3, :, :], ALU.add)
            ptq = rbank(BF16, [128, 128])
            c.transpose(ptq, qe.re("p h two i -> p (h two i)"), K["ident"])
            c.copy(qp[:, tt * 128:(tt + 1) * 128], ptq)

    def flash(qb):
        t0 = qb * TQ
        qn = qnT[qb % 2]
        qp = qpT[qb % 2]
        for h in range(2):
            kr = krA if h == 0 else krB
            nkt = (t0 + TQ) // 128
            po = c.bank(4 + blk[0] % 2, F32, [128, TQ])
            pl = c.bank(6 + blk[0] % 2, F32, [128, TQ])

            def scores(kt):
                c0 = max(0, kt * 128 - t0)
                ps = rbank(F32, [128, TQ])
                c.mm(ps[:, c0:TQ], knT[:, h, kt * 128:(kt + 1) * 128], qn[:, h, c0:TQ], start=True, stop=False)
                c.mm(ps[:, c0:TQ], kr[:, kt * 128:(kt + 1) * 128], qp[:, c0:TQ], start=False, stop=True)
                return ps, c0

            nxt = scores(0)
            for kt in range(nkt):
                ps, c0 = nxt
                if kt + 1 < nkt:
                    nxt = scores(kt + 1)
                pt = PT[kt % 4]
                c.act(pt[:, c0:TQ], ps[:, c0:TQ], AF.Exp, scale=scale)
                if kt * 128 >= t0:
                    c.memset(pt[64:128, c0:c0 + 64], 0.0)
                c.mm(po[:, c0:TQ], Vt[:, kt, h * 128:(h + 1) * 128], pt[:, c0:TQ], start=(kt == 0), stop=(kt == nkt - 1))
                c.mm(pl[:, c0:TQ], ones, pt[:, c0:TQ], start=(kt == 0), stop=(kt == nkt - 1))
            r = rl[blk[0] % 2]
            o = oT[blk[0] % 2]
            c.recip(r, pl)
            c.tt(o, po, r, ALU.mult)
            c.dma(out_dst(h, t0, TQ), o)
            blk[0] += 1
    prologue(0)
    for qb in range(nQB):
        if qb + 1 < nQB:
            prologue(qb + 1)
        flash(qb)
    c.release()
    c.same = True


def build_p2_mla(S):
    c = Ctx()
    latT = c.dram("latT", [1088, S], BF16, "ExternalInput")
    wq = c.dram("wq", [512, 384], F32, "ExternalInput")
    wkv = c.dram("wkv", [512, 512], F32, "ExternalInput")
    rope = c.dram("rope_cs", [S, 64], F32, "ExternalInput")
    ident = c.dram("ident", [128, 128], F32, "ExternalInput")
    out = c.dram("mlaT", [256, S], BF16, "ExternalOutput")
    K = setup_consts(c, ident)
    mla_block(c, K, lambda t0, n: latT[0:1024, t0:t0 + n], lambda t0, n: latT[1024:1088, t0:t0 + n], wq, wkv, rope,
              lambda h, t0, n: out[h * 128:(h + 1) * 128, t0:t0 + n], S)
    c.finish()
    c.emit()
    return c


def mla_weights(w_uq, w_ukv, core):
    h0 = 2 * core
    q = w_uq.reshape(512, 16, 192)
    kv = w_ukv.reshape(512, 16, 256)
    wq = np.concatenate([q[:, h0, :128], q[:, h0 + 1, :128], q[:, h0, 128:], q[:, h0 + 1, 128:]], axis=1)
    wkv = np.concatenate([kv[:, h0, :128], kv[:, h0 + 1, :128], kv[:, h0, 128:], kv[:, h0 + 1, 128:]], axis=1)
    return np.ascontiguousarray(wq), np.ascontiguousarray(wkv)


def run_pipeline(starters, stagger):
    active = []
    nxt = 0
    since = stagger
    while active or nxt < len(starters):
        if nxt < len(starters) and since >= stagger and len(active) < 3:
            active.append(starters[nxt]())
            nxt += 1
            since = 0
        still = []
        for gens in active:
            alive = []
            for g in gens:
                try:
                    next(g)
                    alive.append(g)
                except StopIteration:
                    pass
            if alive:
                still.append(alive)
        active = still
        since += 1
        if not active:
            since = stagger


HG_STAGGER = 5
MB_STAGGER = 3


def hgrn_consts():
    s = np.arange(128)
    same = (s[:, None] // 64) == (s[None, :] // 64)
    tri = (same & (s[:, None] <= s[None, :])).astype(np.float32)
    mref = (same & ((s[:, None] % 64) <= 31)).astype(np.float32)
    ones = same.astype(np.float32)
    cind = np.stack([(s < 64), (s >= 64)], axis=1).astype(np.float32)
    m = np.zeros((128, 512), np.float32)
    m[:, 0:128] = tri
    m[:, 128:256] = tri - mref
    m[:, 256:384] = ones - tri
    m[:, 384:386] = cind
    return m


def hgrn_block(c, K, hT_src, wh_d, lg_d, nw_d, hc_d, out_dst, S, layer):
    TB = min(512, S)
    nB = S // TB
    nTT = TB // 128
    c.same = "auto"
    c.mark()
    wh = c.sbuf("g_wh", [128, KC, 1024], BF16)
    for h in range(2):
        c.dma(wh[:, :, h * 512:(h + 1) * 512], wh_d[:, h * 512:(h + 1) * 512].re("(kc p) n -> p kc n", p=128), q="pool")
    hc = c.sbuf("g_hc", [128, 512], F32)
    c.dma(hc, hc_d)
    tri = hc[:, 0:128]
    lg = c.sbuf("g_lg", [128, 2, 256], F32)
    c.dma(lg, lg_d.wap(lg_d.ap.rearrange("l n -> (l n)").partition_broadcast(128)).re("p (l n) -> p l n", l=2))
    ee = c.sbuf("g_ee", [128, 2, 256], F32)
    c.act(ee, lg, AF.Exp)
    den = c.sbuf("g_den", [128, 256], F32)
    c.tt(den, ee[:, 0, :], ee[:, 1, :], ALU.add)
    c.recip(den, den)
    lb = c.sbuf("g_lb", [128, 256], F32)
    oml = c.sbuf("g_oml", [128, 256], F32)
    if layer == 0:
        c.ts(lb, ee[:, 0, :], 0.0, ALU.mult)
    else:
        c.tt(lb, ee[:, 1, :], den, ALU.mult)
    c.ts(oml, lb, -1.0, ALU.mult, 1.0, ALU.add)
    nwb = c.sbuf("g_nwb", [128, 256], F32)
    c.dma(nwb, nw_d.wap(nw_d.ap.partition_broadcast(128)))
    hTb = [c.sbuf("g_hT%d" % i, [128, KC, TB], BF16) for i in range(2)]
    ost = [[c.sbuf("g_ost%d_%d" % (h, i), [128, TB], BF16) for i in range(2)] for h in range(2)]

    class HS:
        pass

    states = []
    for h in range(2):
        stt_ = HS()
        stt_.S = [c.sbuf("g_S%d_%d" % (h, i), [128, 128], F32) for i in range(2)]
        stt_.Sb = [c.sbuf("g_Sb%d_%d" % (h, i), [128, 128], BF16) for i in range(2)]
        c.memset(stt_.S[0], 0.0)
        c.memset(stt_.Sb[0], 0.0)
        states.append(stt_)
    hs = {}
    for h in range(2):
      for par in range(2):
        o = HS()
        o.state = states[h]
        n = "g%d%d_" % (h, par)
        o.f = c.sbuf(n + "f", [128, 128], F32)
        o.E4 = c.sbuf(n + "E4", [128, 512], F32)
        o.osb = c.sbuf(n + "osb", [128, 128], F32)
        o.glog = c.sbuf(n + "glog", [128, 128], F32)
        o.kk = c.sbuf(n + "kk", [128, 128], F32)
        o.E3 = c.sbuf(n + "E3", [128, 3, 128], F32)
        o.ek = c.sbuf(n + "ek", [128, 128], F32)
        o.ebl = c.sbuf(n + "ebl", [128, 2], F32)
        o.qs = c.sbuf(n + "qs", [128, 128], F32)
        o.gw = c.sbuf(n + "gw", [128, 128], F32)
        o.tok = c.sbuf(n + "tok", [128, 3, 128], BF16)
        o.kdp = [c.sbuf(n + "kdp%d" % i, [128, 128], BF16) for i in range(2)]
        o.v = c.sbuf(n + "v", [128, 128], BF16)
        o.qT = c.sbuf(n + "qT", [128, 128], BF16)
        o.qbp = [c.sbuf(n + "qbp%d" % i, [128, 128], BF16) for i in range(2)]
        o.kTp = [c.sbuf(n + "kTp%d" % i, [128, 128], BF16) for i in range(2)]
        o.ATm = c.sbuf(n + "ATm", [128, 128], BF16)
        o.sq = c.sbuf(n + "sq", [128, 128], F32)
        o.st = c.sbuf(n + "st", [128, 4], F32)
        o.y = c.sbuf(n + "y", [128, 128], BF16)
        for t_ in o.kdp + o.qbp + o.kTp:
            c.memset(t_, 0.0)
        b0 = 4 * h
        o.proj = c.bank(b0)
        o.bb = c.bank(b0 + 1, F32, [128, 384])
        o.pb4 = c.bank(b0 + 1)
        o.pb4 = o.pb4.wap(o.pb4.ap[:, 384:386])
        b2 = c.banks[b0 + 2]
        o.ptr = V(b2.ap.bitcast(BF16)[:, 0:384].rearrange("p (a b) -> p a b", a=3), b2.bufs)
        o.pyT = V(b2.ap.bitcast(BF16)[:, 384:512], b2.bufs)
        o.pS = [V(b2.ap[:, 256:384], b2.bufs), V(b2.ap[:, 384:512], b2.bufs)]
        o.pAT = c.bank(b0 + 3, F32, [128, 128])
        b3 = c.banks[b0 + 3]
        o.po = V(b3.ap[:, 128:256], b3.bufs)
        hs[(h, par)] = o

    def tile_gen(h, hTt, tt, ostage, par, fin):
        o = hs[(h, par)]
        tsl = slice(tt * 128, (tt + 1) * 128)
        for kc in range(KC):
            c.mm(o.proj, hTt[:, kc, tsl], wh[:, kc, h * 512:(h + 1) * 512], start=(kc == 0), stop=(kc == KC - 1))
        q_in, f_in, i_in, g_in = (o.proj[:, j * 128:(j + 1) * 128] for j in range(4))
        hsl = slice(h * 128, (h + 1) * 128)
        c.act(o.E4, o.proj, AF.Exp, scale=-1.0)
        c.copy(o.v, i_in, eng="act")
        c.ts(o.E4, o.E4, 1.0, ALU.add)
        c.recip(o.E4, o.E4)
        c.tt(o.qs, q_in, o.E4[:, 0:128], ALU.mult)
        c.tt(o.gw, g_in, o.E4[:, 384:512], ALU.mult)
        c.tt(o.f, o.E4[:, 128:256], oml[:, hsl], ALU.mult)
        c.tt(o.f, o.f, lb[:, hsl], ALU.add)
        c.act(o.glog, o.f, AF.Ln)
        c.ts(o.kk, o.f, -1.0, ALU.mult, 1.0, ALU.add, eng="pool")
        c.tt(o.gw, o.gw, nwb[:, hsl], ALU.mult, eng="pool")
        yield
        for j in range(3):
            c.mm(o.bb[:, j * 128:(j + 1) * 128], hc[:, j * 128:(j + 1) * 128], o.glog)
        c.mm(o.pb4, o.glog, hc[:, 384:386])
        c.act(o.E3, o.bb.re("p (a b) -> p a b", a=3), AF.Exp)
        c.act(o.ek, o.bb[:, 128:256], AF.Exp, scale=-1.0)
        c.act(o.ebl, o.pb4, AF.Exp)
        sc = 128.0 ** -0.5
        c.stt(o.tok[:, 0, :], o.qs, sc, o.E3[:, 1, :], ALU.mult, ALU.mult)
        c.stt(o.tok[:, 1, :], o.qs, sc, o.E3[:, 0, :], ALU.mult, ALU.mult)
        c.tt(o.tok[:, 2, :], o.kk, o.ek, ALU.mult, eng="pool")
        c.tt(o.kdp[0][0:64, :], o.kk[0:64, :], o.E3[0:64, 2, :], ALU.mult, eng="pool")
        c.tt(o.kdp[1][64:128, :], o.kk[64:128, :], o.E3[64:128, 2, :], ALU.mult, eng="pool")
        yield
        for j in range(3):
            c.transpose(o.ptr[:, j, :], o.tok[:, j, :], K["ident"])
        c.copy(o.qT, o.ptr[:, 0, :], eng="act")
        c.copy(o.qbp[0][:, 0:64], o.ptr[:, 1, 0:64])
        c.copy(o.qbp[1][:, 64:128], o.ptr[:, 1, 64:128])
        c.copy(o.kTp[0][:, 0:64], o.ptr[:, 2, 0:64])
        c.copy(o.kTp[1][:, 64:128], o.ptr[:, 2, 64:128], eng="act")
        yield
        c.mm(o.pAT[:, 0:64], o.kTp[0], o.qT[:, 0:64])
        c.mm(o.pAT[:, 64:128], o.kTp[1], o.qT[:, 64:128])
        c.tt(o.ATm, o.pAT, tri, ALU.mult)
        S0, S1 = o.state.S[0], o.state.S[1]
        Sb0, Sb1 = o.state.Sb[0], o.state.Sb[1]
        c.mm(o.pS[0], o.kdp[0], o.v)
        c.stt(S1, S0, o.ebl[:, 0:1], o.pS[0], ALU.mult, ALU.add)
        c.copy(Sb1, S1, eng="act")
        yield
        c.mm(o.po, o.ATm, o.v, start=True, stop=False)
        c.mm(o.po, o.qbp[0], Sb0, start=False, stop=False)
        c.mm(o.po, o.qbp[1], Sb1, start=False, stop=True)
        c.mm(o.pS[1], o.kdp[1], o.v)
        c.stt(S0, S1, o.ebl[:, 1:2], o.pS[1], ALU.mult, ALU.add)
        c.copy(Sb0, S0, eng="act")
        yield
        c.copy(o.osb, o.po, eng="act")
        c.tt(o.sq, o.osb, o.osb, ALU.mult, eng="pool")
        c.reduce(o.st[:, 0:1], o.sq)
        c.act(o.st[:, 1:2], o.st[:, 0:1], AF.Ln, scale=1.0 / 128, bias=K["eps"])
        c.act(o.st[:, 2:3], o.st[:, 1:2], AF.Exp, scale=-0.5)
        c.stt(o.y, o.osb, o.st[:, 2:3], o.gw, ALU.mult, ALU.mult)
        yield
        c.transpose(o.pyT, o.y, K["ident"])
        c.copy(ostage[:, tsl], o.pyT)
        if fin is not None:
            fin()
        yield

    starters = []
    gi = 0
    for b in range(nB):
        for tt in range(nTT):
            def start(b=b, tt=tt, gi=gi):
                hTt = hTb[b % 2]
                if tt == 0:
                    c.dma(hTt, hT_src(b * TB, TB).re("(kc p) t -> p kc t", p=128))
                gl = []
                for h in range(2):
                    fin = None
                    if tt == nTT - 1:
                        fin = (lambda h=h, b=b: c.dma(out_dst(h, b * TB, TB), ost[h][b % 2]))
                    gl.append(tile_gen(h, hTt, tt, ost[h][b % 2], gi % 2, fin))
                return gl
            starters.append(start)
            gi += 1
    run_pipeline(starters, HG_STAGGER)
    c.release()
    c.same = True


def build_p2_hgrn(S, layer):
    c = Ctx()
    hT = c.dram("hT", [D, S], BF16, "ExternalInput")
    wh = c.dram("wh", [D, 1024], F32, "ExternalInput")
    lg = c.dram("lg", [2, 256], F32, "ExternalInput")
    nw = c.dram("hnw", [256], F32, "ExternalInput")
    hc = c.dram("hc", [128, 512], F32, "ExternalInput")
    ident = c.dram("ident", [128, 128], F32, "ExternalInput")
    out = c.dram("hgT", [256, S], BF16, "ExternalOutput")
    K = setup_consts(c, ident)
    hgrn_block(c, K, lambda t0, n: hT[:, t0:t0 + n], wh, lg, nw, hc, lambda h, t0, n: out[h * 128:(h + 1) * 128, t0:t0 + n], S, layer)
    c.finish()
    c.emit()
    return c


def hgrn_weights(w_in, core):
    cols = []
    for h in (2 * core, 2 * core + 1):
        for off in (O_HQ, O_HF, O_HI, O_HG):
            cols.append(w_in[:, off + h * 128: off + (h + 1) * 128])
    return np.ascontiguousarray(np.concatenate(cols, axis=1))


def mamba_consts():
    m = np.zeros((128, 512), np.float32)
    s = np.arange(128)
    same = (s[:, None] // 64) == (s[None, :] // 64)
    tri = (same & (s[:, None] <= s[None, :])).astype(np.float32)
    m[:, 0:128] = 1.0
    m[:, 128:256] = -tri
    m[:, 256:384] = (s[:, None] < 64) * 1.0 + 0.0 * s[None, :]
    m[:, 384:512] = (s[:, None] >= 64) * 1.0 + 0.0 * s[None, :]
    return m


def mamba_block(c, K, hT_src, wm_d, cw_d, cb_d, sp_d, nw_d, hc_d, mc_d, out_dst, S):
    TB = min(512, S)
    nB = S // TB
    nTT = TB // 128
    c.mark()
    wm = c.sbuf("s_wm", [128, KC, 1288], BF16)
    for (a, b) in ((0, 512), (512, 1024), (1024, 1288)):
        c.dma(wm[:, :, a:b], wm_d[:, a:b].re("(kc p) n -> p kc n", p=128), q="pool")
    hc = c.sbuf("s_hc", [128, 512], F32)
    c.dma(hc, hc_d)
    mc = c.sbuf("s_mc", [128, 512], F32)
    c.dma(mc, mc_d)
    tri = hc[:, 0:128]
    omt = hc[:, 256:384]
    ones_full = mc[:, 0:128]
    ntri = mc[:, 128:256]
    cw = c.sbuf("s_cw", [128, 6, 4], F32)
    for j in range(4):
        c.dma(cw[:, :, j], cw_d[j, :].re("(cc p) -> p cc", p=128), slow=True)
    cbias = load_cols(c, "s_cb", cb_d, 6)
    spb = c.sbuf("s_spb", [128, 3, 8], F32)
    c.dma(spb, sp_d.wap(sp_d.ap.rearrange("a h -> (a h)").partition_broadcast(128)).re("p (a h) -> p a h", a=3))
    abc = c.sbuf("s_abc", [128, 8], F32)
    c.act(abc, spb[:, 0, :], AF.Exp)
    c.ts(abc, abc, -1.0, ALU.mult)
    one = c.sbuf("s_one", [128, 1], F32)
    c.memset(one, 1.0)
    nwb = c.sbuf("s_nwb", [128, 512], F32)
    c.dma(nwb, nw_d.wap(nw_d.ap.partition_broadcast(128)))
    hTb = [c.sbuf("s_hT%d" % i, [128, KC, TB], BF16) for i in range(2)]
    xpre = c.sbuf("s_xpre", [128, 6, TB + 4], F32)
    c.memset(xpre, 0.0)
    cacc = [c.sbuf("s_cacc%d" % i, [128, TB], F32) for i in range(2)]
    xcTs = [c.sbuf("s_xcT%d" % i, [128, 6, TB], BF16) for i in range(2)]
    ost = [c.sbuf("s_ost%d" % i, [128, 4, TB], BF16) for i in range(2)]
    HT = c.sbuf("s_HT", [128, 8, 64], F32)
    HTb = [c.sbuf("s_HTb%d" % i, [128, 512], BF16) for i in range(2)]
    c.memset(HT, 0.0)
    c.memset(HTb[0], 0.0)

    class SC:
        pass

    scr = []
    for par in range(2):
        o = SC()
        n = "s%d_" % par
        o.xs_tm = c.sbuf(n + "xs", [128, 8, 64], BF16)
        o.Bp = [c.sbuf(n + "Bp%d" % i, [128, 128], BF16) for i in range(2)]
        o.CTp = [c.sbuf(n + "CTp%d" % i, [128, 128], BF16) for i in range(2)]
        for t_ in o.Bp + o.CTp:
            c.memset(t_, 0.0)
        o.sm = c.sbuf(n + "sm", [128, 8, 8], F32)
        o.dtA = c.sbuf(n + "dtA", [128, 8], F32)
        o.rhsR = c.sbuf(n + "rhsR", [128, 8, 128], F32)
        o.dtAb = c.sbuf(n + "dtAb", [128, 8, 128], F32)
        o.segm = c.sbuf(n + "segm", [128, 8, 128], F32)
        o.Dm = c.sbuf(n + "Dm", [128, 8, 128], F32)
        o.cbm = c.sbuf(n + "cbm", [128, 128], F32)
        o.MT = c.sbuf(n + "MT", [128, 8, 128], BF16)
        o.xdt = c.sbuf(n + "xdt", [128, 8, 64], BF16)
        o.xdd = c.sbuf(n + "xdd", [128, 8, 64], BF16)
        o.sz = c.sbuf(n + "sz", [128, 512], F32)
        o.y1 = c.sbuf(n + "y1", [128, 8, 64], F32)
        o.sq = c.sbuf(n + "sq", [128, 512], F32)
        o.st = c.sbuf(n + "st", [128, 4], F32)
        o.yb = c.sbuf(n + "yb", [128, 512], BF16)
        o.cd = c.sbuf(n + "cd", [128, 2, 8], F32)
        scr.append(o)

    bk_tr = c.banks[2]
    ptr = V(bk_tr.ap.bitcast(BF16)[:, 0:640].rearrange("p (a b) -> p a b", a=5), bk_tr.bufs)
    b3 = c.banks[3]

    def prologue(b):
        hTt = hTb[b % 2]
        xcT = xcTs[b % 2]
        c.dma(hTt, hT_src(b * TB, TB).re("(kc p) t -> p kc t", p=128))
        for cc in range(6):
            pp = c.bank(cc % 2, F32, [128, TB])
            for kc in range(KC):
                c.mm(pp, wm[:, kc, 512 + cc * 128:512 + (cc + 1) * 128], hTt[:, kc, :], start=(kc == 0), stop=(kc == KC - 1))
            c.copy(xpre[:, cc, 3:3 + TB], pp, eng="act")
            ac = cacc[cc % 2]
            c.ts(ac, xpre[:, cc, 0:TB], cw[:, cc, 0:1], ALU.mult)
            for j in (1, 2, 3):
                c.stt(ac, xpre[:, cc, j:j + TB], cw[:, cc, j:j + 1], ac, ALU.mult, ALU.add)
            c.act(xcT[:, cc, :], ac, AF.Silu, bias=cbias[:, cc:cc + 1])
            c.copy(xpre[:, cc, 0:3], xpre[:, cc, TB:TB + 3], eng="pool")

    def tile_gen(b, tt, par):
        o = scr[par]
        hTt = hTb[b % 2]
        xcT = xcTs[b % 2]
        sm, dtA = o.sm, o.dtA
        tsl = slice(tt * 128, (tt + 1) * 128)
        pz = c.bank(tt % 2, F32, [128, 512])
        for kc in range(KC):
            c.mm(pz, hTt[:, kc, tsl], wm[:, kc, 0:512], start=(kc == 0), stop=(kc == KC - 1))
        pdt = V(b3.ap[:, 0:8], b3.bufs)
        for kc in range(KC):
            c.mm(pdt, hTt[:, kc, tsl], wm[:, kc, 1280:1288], start=(kc == 0), stop=(kc == KC - 1))
        c.act(o.sz, pz, AF.Silu)
        c.tt(sm[:, 0, :], pdt, spb[:, 1, :], ALU.add)
        c.act(sm[:, 1, :], sm[:, 0, :], AF.Exp)
        c.act(sm[:, 2, :], sm[:, 1, :], AF.Ln, bias=one)
        c.tt(dtA, sm[:, 2, :], abc, ALU.mult)
        for j in range(5):
            c.transpose(ptr[:, j, :], xcT[:, j, tsl], K["ident"])
        c.copy(o.xs_tm.re("p h q -> p (h q)"), ptr[:, 0:4, :].re("p a b -> p (a b)"), eng="act")
        c.copy(o.Bp[0][0:64, :], ptr[0:64, 4, :])
        c.copy(o.Bp[1][64:128, :], ptr[64:128, 4, :])
        c.copy(o.CTp[0][:, 0:64], xcT[:, 5, tt * 128:tt * 128 + 64], eng="pool")
        c.copy(o.CTp[1][:, 64:128], xcT[:, 5, tt * 128 + 64:tt * 128 + 128], eng="pool")
        yield
        pac = V(b3.ap[:, 8:16], b3.bufs)
        plm = V(b3.ap[:, 16:24], b3.bufs)
        pcd0 = V(b3.ap[:, 24:32], b3.bufs)
        pcd1 = V(b3.ap[:, 32:40], b3.bufs)
        pcb = V(b3.ap[:, 128:256], b3.bufs)
        c.mm(pac, tri, dtA)
        c.mm(plm, omt, dtA)
        c.mm(pcd0, mc[:, 256:384], dtA)
        c.mm(pcd1, mc[:, 384:512], dtA)
        c.mm(pcb, xcT[:, 4, tsl], xcT[:, 5, tsl])
        c.act(sm[:, 3:7, :].re("p a h -> p (a h)"), V(b3.ap[:, 8:40], b3.bufs), AF.Exp)
        c.tt(o.cbm, pcb, tri, ALU.mult)
        dbc = dtA.wap(dtA.ap.unsqueeze(2).broadcast_to([128, 8, 128]))
        tbc = tri.wap(tri.ap.unsqueeze(1).broadcast_to([128, 8, 128]))
        c.tt(o.rhsR, dbc, tbc, ALU.mult)
        c.copy(o.dtAb, dbc, eng="pool")
        yield
        for half in range(2):
            pseg = c.bank(4 + half)
            c.mm(pseg, ones_full, o.rhsR[:, half * 4:(half + 1) * 4, :].re("p a b -> p (a b)"), start=True, stop=False)
            c.mm(pseg, ntri, o.dtAb[:, half * 4:(half + 1) * 4, :].re("p a b -> p (a b)"), start=False, stop=True)
            c.ts(o.segm[:, half * 4:(half + 1) * 4, :].re("p a b -> p (a b)"), pseg, 0.0, ALU.min)
        c.act(o.Dm, o.segm, AF.Exp)
        c.tt(o.MT, o.Dm, o.cbm.wap(o.cbm.ap.unsqueeze(1).broadcast_to([128, 8, 128])), ALU.mult)
        dtb = sm[:, 2, :].wap(sm.ap[:, 2, :].unsqueeze(2).broadcast_to([128, 8, 64]))
        c.tt(o.xdt, o.xs_tm, dtb, ALU.mult)
        dsb = sm[:, 4, :].wap(sm.ap[:, 4, :].unsqueeze(2).broadcast_to([128, 8, 64]))
        c.tt(o.xdd, o.xdt, dsb, ALU.mult, eng="pool")
        yield
        pyd = c.bank(6)
        for h in range(8):
            c.mm(pyd[:, h * 64:(h + 1) * 64], o.MT[:, h, :], o.xdt[:, h, :])
        yield
        pyo = c.bank(7)
        pH = c.bank(7)
        y1 = o.y1
        cd = sm[:, 5:7, :]
        c.mm(pyo, o.CTp[0], HTb[0], start=True, stop=True)
        eab = sm[:, 3, :].wap(sm.ap[:, 3, :].unsqueeze(2).broadcast_to([128, 8, 64]))
        c.tt(y1[0:64], pyo[0:64, :].re("p (h q) -> p h q", h=8), eab[0:64], ALU.mult)
        c.mm(pH, o.Bp[0], o.xdd.re("p h q -> p (h q)"))
        c.tt(HT, HT, cd[:, 0, :].wap(cd.ap[:, 0, :].unsqueeze(2).broadcast_to([128, 8, 64])), ALU.mult)
        c.tt(HT, HT, pH.re("p (h q) -> p h q", h=8), ALU.add)
        c.copy(HTb[1], HT.re("p h q -> p (h q)"), eng="act")
        yield
        c.mm(pyo, o.CTp[1], HTb[1], start=True, stop=True)
        c.tt(y1[64:128], pyo[64:128, :].re("p (h q) -> p h q", h=8), eab[64:128], ALU.mult)
        c.mm(pH, o.Bp[1], o.xdd.re("p h q -> p (h q)"))
        c.tt(HT, HT, cd[:, 1, :].wap(cd.ap[:, 1, :].unsqueeze(2).broadcast_to([128, 8, 64])), ALU.mult)
        c.tt(HT, HT, pH.re("p (h q) -> p h q", h=8), ALU.add)
        c.copy(HTb[0], HT.re("p h q -> p (h q)"), eng="act")
        yield
        y1f = y1.re("p h q -> p (h q)")
        sq, st = o.sq, o.st
        c.tt(y1f, pyd, y1f, ALU.add)
        dskb = spb[:, 2, :].wap(spb.ap[:, 2, :].unsqueeze(2).broadcast_to([128, 8, 64]))
        c.tt(sq.re("p (h q) -> p h q", h=8), o.xs_tm, dskb, ALU.mult, eng="pool")
        c.tt(y1f, y1f, sq, ALU.add)
        c.tt(y1f, y1f, o.sz, ALU.mult)
        c.tt(sq, y1f, y1f, ALU.mult, eng="pool")
        c.reduce(st[:, 0:1], sq)
        c.act(st[:, 1:2], st[:, 0:1], AF.Ln, scale=1.0 / 512, bias=K["eps"])
        c.act(st[:, 2:3], st[:, 1:2], AF.Exp, scale=-0.5)
        c.stt(o.yb, y1f, st[:, 2:3], nwb, ALU.mult, ALU.mult)
        yield
        for j in range(4):
            c.transpose(ptr[:, j, :], o.yb[:, j * 128:(j + 1) * 128], K["ident"])
        c.copy(ost[b % 2][:, :, tsl], ptr[:, 0:4, :], eng="act")
        if tt == nTT - 1:
            c.dma(out_dst(b * TB, TB).re("(cc p) t -> p cc t", p=128), ost[b % 2])
        yield

    starters = []
    gi = 0
    for b in range(nB):
        for tt in range(nTT):
            def start(b=b, tt=tt, gi=gi):
                if tt == 0:
                    prologue(b)
                return [tile_gen(b, tt, gi % 2)]
            starters.append(start)
            gi += 1
    run_pipeline(starters, MB_STAGGER)
    c.release()


def build_p2_mamba(S):
    c = Ctx()
    hT = c.dram("hT", [D, S], BF16, "ExternalInput")
    wm = c.dram("wm", [D, 1288], F32, "ExternalInput")
    cw = c.dram("cw", [4, 768], F32, "ExternalInput")
    cb = c.dram("cb", [768], F32, "ExternalInput")
    sp = c.dram("sp", [3, 8], F32, "ExternalInput")
    nw = c.dram("snw", [512], F32, "ExternalInput")
    hc = c.dram("hc", [128, 512], F32, "ExternalInput")
    mc = c.dram("mc", [128, 512], F32, "ExternalInput")
    ident = c.dram("ident", [128, 128], F32, "ExternalInput")
    out = c.dram("ssT", [512, S], BF16, "ExternalOutput")
    K = setup_consts(c, ident)
    mamba_block(c, K, lambda t0, n: hT[:, t0:t0 + n], wm, cw, cb, sp, nw, hc, mc, lambda t0, n: out[:, t0:t0 + n], S)
    c.finish()
    c.emit()
    return c


def mamba_inputs(inp, l, g):
    w_in = inp["w_in"][l]
    xo = O_XBC
    wm = np.concatenate([w_in[:, O_Z + g * 512:O_Z + (g + 1) * 512], w_in[:, xo + g * 512:xo + (g + 1) * 512],
                         w_in[:, xo + 4096 + g * 128:xo + 4096 + (g + 1) * 128],
                         w_in[:, xo + 5120 + g * 128:xo + 5120 + (g + 1) * 128],
                         w_in[:, O_DT + g * 8:O_DT + (g + 1) * 8]], axis=1)
    sel = np.r_[g * 512:(g + 1) * 512, 4096 + g * 128:4096 + (g + 1) * 128, 5120 + g * 128:5120 + (g + 1) * 128]
    cw = inp["ssm_conv_w"][l][:, sel]
    cb = inp["ssm_conv_b"][l][sel]
    sp = np.stack([inp["ssm_a_log"][l][g * 8:(g + 1) * 8], inp["ssm_dt_bias"][l][g * 8:(g + 1) * 8],
                   inp["ssm_d"][l][g * 8:(g + 1) * 8]], axis=0)
    nw = inp["ssm_norm"][l][g * 512:(g + 1) * 512]
    return dict(wm=np.ascontiguousarray(wm), cw=np.ascontiguousarray(cw), cb=np.ascontiguousarray(cb),
                sp=np.ascontiguousarray(sp), snw=np.ascontiguousarray(nw))


def merge_block(c, K, x1, mix_src, x2, mix_nw, wg_d, woa_d, wob_d, woc_d, wout_d, TOK, cmap=None):
    TG = min(512, TOK)
    nH = TOK // TG
    nTT = TG // 128
    c.mark()
    hTh = c.sbuf("x_hTh", [128, KC, TG], BF16)
    mixh = c.sbuf("x_mix", [128, 64, TG], BF16)
    uT = c.sbuf("x_uT", [128, KC, TG], BF16)
    nwc = load_cols(c, "x_nwc", mix_nw, KC)
    comps = ((woa_d, 0, 16), (wob_d, 16, 16), (woc_d, 32, 32))
    for th in range(nH):
        t0 = th * TG
        c.dma(mixh, mix_src(t0, TG).re("(j p) t -> p j t", p=128))
        c.mark()
        norm_transpose(c, K, x1[t0:t0 + TG, :], nTT, hTh, nwc, tag="x_nt")
        c.release()
        c.mark()
        NB = 10
        wp = [c.sbuf("x_wp%d" % i, [128, 16, 256], BF16) for i in range(NB)]
        sg = [c.sbuf("x_sg%d" % i, [128, TG], F32) for i in range(2)]
        acc = [c.sbuf("x_acc%d" % i, [128, TG], F32) for i in range(2)]
        wi_ = [0]

        def nextbuf():
            b = wp[wi_[0] % NB]
            wi_[0] += 1
            return b

        it = 0
        for pr in range(D // 256):
            blks = []
            for ci, (wo_d, base, nch) in enumerate(comps):
                gb = nextbuf()
                c.dma(gb, wg_d[:, ci * D + pr * 256:ci * D + (pr + 1) * 256].re("(kc p) n -> p kc n", p=128), q="pool")
                wbs = []
                for part in range(nch // 16):
                    wb = nextbuf()
                    c.dma(wb, wo_d[part * 2048:(part + 1) * 2048, pr * 256:(pr + 1) * 256].re("(kc p) n -> p kc n", p=128),
                          q="pool")
                    wbs.append(wb)
                blks.append((gb, wbs, base, nch))
            for j in range(2):
                dcn = pr * 2 + j
                a = acc[dcn % 2]
                for ci, (gb, wbs, base, nch) in enumerate(blks):
                    pg = c.bank(it % 2, F32, [128, TG])
                    py = c.bank(2 + it % 2, F32, [128, TG])
                    s = sg[it % 2]
                    it += 1
                    for kc in range(KC):
                        c.mm(pg, gb[:, kc, j * 128:(j + 1) * 128], hTh[:, kc, :], start=(kc == 0), stop=(kc == KC - 1))
                    for ch in range(nch):
                        c.mm(py, wbs[ch // 16][:, ch % 16, j * 128:(j + 1) * 128],
                             mixh[:, (base + ch) if cmap is None else cmap(ci, ch), :],
                             start=(ch == 0), stop=(ch == nch - 1))
                    c.act(s, pg, AF.Sigmoid)
                    if ci == 0:
                        c.tt(a, py, s, ALU.mult)
                    elif ci == 1:
                        c.tt(s, py, s, ALU.mult)
                        c.tt(a, a, s, ALU.add, eng="pool")
                    else:
                        c.tt(s, py, s, ALU.mult)
                        c.tt(uT[:, dcn, :], a, s, ALU.add)
        c.release()
        c.mark()
        wob = [c.sbuf("x_wout%d" % i, [128, KC, 512], BF16) for i in range(2)]
        xr = [c.sbuf("x_xr%d" % i, [128, 512], F32) for i in range(4)]
        k = 0
        for dc in range(D // 512):
            wt = wob[dc % 2]
            c.dma(wt, wout_d[:, dc * 512:(dc + 1) * 512].re("(kc p) n -> p kc n", p=128), q="pool")
            for t in range(nTT):
                r = xr[k % 4]
                c.dma(r, x1[t0 + t * 128:t0 + (t + 1) * 128, dc * 512:(dc + 1) * 512])
                po = c.bank(4 + k % 4)
                k += 1
                for kc in range(KC):
                    c.mm(po, uT[:, kc, t * 128:(t + 1) * 128], wt[:, kc, :], start=(kc == 0), stop=(kc == KC - 1))
                c.tt(r, po, r, ALU.add)
                c.dma(x2[t0 + t * 128:t0 + (t + 1) * 128, dc * 512:(dc + 1) * 512], r)
        c.release()
    c.release()


def build_p3(TOK, last):
    c = Ctx()
    x1 = c.dram("x1", [TOK, D], F32, "ExternalInput")
    mixT = c.dram("mixT", [8192, TOK], BF16, "ExternalInput")
    mnw = c.dram("mix_nw", [D], F32, "ExternalInput")
    wg = c.dram("wg", [D, 3 * D], F32, "ExternalInput")
    woa = c.dram("woa", [2048, D], F32, "ExternalInput")
    wob = c.dram("wob", [2048, D], F32, "ExternalInput")
    woc = c.dram("woc", [4096, D], F32, "ExternalInput")
    wout = c.dram("wout", [D, D], F32, "ExternalInput")
    nw = c.dram("ffn_nw", [D], F32, "ExternalInput")
    wi = c.dram("ffn_wi", [D, 2 * DFF], F32, "ExternalInput")
    wo = c.dram("ffn_wo", [DFF, D], F32, "ExternalInput")
    ident = c.dram("ident", [128, 128], F32, "ExternalInput")
    x2 = c.dram("x2", [TOK, D], F32, "Internal")
    if last:
        fnw = c.dram("final_nw", [D], F32, "ExternalInput")
        x3 = c.dram("x3", [TOK, D], F32, "Internal")
        out = c.dram("out", [TOK, D], F32, "ExternalOutput")
    else:
        x3 = c.dram("out", [TOK, D], F32, "ExternalOutput")
    K = setup_consts(c, ident)
    merge_block(c, K, x1, lambda t0, n: mixT[:, t0:t0 + n], x2, mnw, wg, woa, wob, woc, wout, TOK)
    ffn_block(c, K, x2, x3, nw, wi, wo, TOK)
    if last:
        final_norm_block(c, K, x3, out, fnw, TOK)
    c.finish()
    c.emit()
    return c


def build_p2(S, layer):
    c = Ctx()
    hT = c.dram("hT", [D, S], BF16, "ExternalInput")
    latT = c.dram("latT", [1088, S], BF16, "ExternalInput")
    wq = c.dram("wq", [512, 384], F32, "ExternalInput")
    wkv = c.dram("wkv", [512, 512], F32, "ExternalInput")
    rope = c.dram("rope_cs", [S, 64], F32, "ExternalInput")
    wh = c.dram("wh", [D, 1024], F32, "ExternalInput")
    lg = c.dram("lg", [2, 256], F32, "ExternalInput")
    hnw = c.dram("hnw", [256], F32, "ExternalInput")
    wm = c.dram("wm", [D, 1288], F32, "ExternalInput")
    cw = c.dram("cw", [4, 768], F32, "ExternalInput")
    cb = c.dram("cb", [768], F32, "ExternalInput")
    sp = c.dram("sp", [3, 8], F32, "ExternalInput")
    snw = c.dram("snw", [512], F32, "ExternalInput")
    hc = c.dram("hc", [128, 512], F32, "ExternalInput")
    mc = c.dram("mc", [128, 512], F32, "ExternalInput")
    ident = c.dram("ident", [128, 128], F32, "ExternalInput")
    out = c.dram("mixT", [1024, S], BF16, "ExternalOutput")
    K = setup_consts(c, ident)
    mla_block(c, K, lambda t0, n: latT[0:1024, t0:t0 + n], lambda t0, n: latT[1024:1088, t0:t0 + n], wq, wkv, rope,
              lambda h, t0, n: out[h * 128:(h + 1) * 128, t0:t0 + n], S)
    hgrn_block(c, K, lambda t0, n: hT[:, t0:t0 + n], wh, lg, hnw, hc,
               lambda h, t0, n: out[256 + h * 128:256 + (h + 1) * 128, t0:t0 + n], S, layer)
    mamba_block(c, K, lambda t0, n: hT[:, t0:t0 + n], wm, cw, cb, sp, snw, hc, mc,
                lambda t0, n: out[512:1024, t0:t0 + n], S)
    c.finish()
    c.emit()
    return c


_CACHE = {}


def _get(key, fn):
    if key not in _CACHE:
        _CACHE[key] = fn()
    return _CACHE[key]


def build_tok(TOK, has_merge, has_p1, last):
    c = Ctx()
    ident = c.dram("ident", [128, 128], F32, "ExternalInput")
    K = setup_consts(c, ident)
    xin = c.dram("xin", [TOK, D], F32, "ExternalInput")
    cur = xin
    if has_merge:
        mixT = c.dram("mixT", [8192, TOK], BF16, "ExternalInput")
        mnw = c.dram("m_mix_nw", [D], F32, "ExternalInput")
        wg = c.dram("wg", [D, 3 * D], F32, "ExternalInput")
        woa = c.dram("woa", [2048, D], F32, "ExternalInput")
        wob = c.dram("wob", [2048, D], F32, "ExternalInput")
        woc = c.dram("woc", [4096, D], F32, "ExternalInput")
        wout = c.dram("wout", [D, D], F32, "ExternalInput")
        nw2 = c.dram("ffn2_nw", [D], F32, "ExternalInput")
        wi2 = c.dram("ffn2_wi", [D, 2 * DFF], F32, "ExternalInput")
        wo2 = c.dram("ffn2_wo", [DFF, D], F32, "ExternalInput")
        x2 = c.dram("x2", [TOK, D], F32, "Internal")
        if last:
            fnw = c.dram("final_nw", [D], F32, "ExternalInput")
            x3 = c.dram("x3", [TOK, D], F32, "Internal")
            out = c.dram("out", [TOK, D], F32, "ExternalOutput")
        else:
            x3 = c.dram("x3", [TOK, D], F32, "Internal")
        merge_block(c, K, cur, lambda t0, n: mixT[:, t0:t0 + n], x2, mnw, wg, woa, wob, woc, wout, TOK)
        ffn_block(c, K, x2, x3, nw2, wi2, wo2, TOK)
        cur = x3
        if last:
            final_norm_block(c, K, x3, out, fnw, TOK)
    if has_p1:
        nw = c.dram("ffn1_nw", [D], F32, "ExternalInput")
        wi = c.dram("ffn1_wi", [D, 2 * DFF], F32, "ExternalInput")
        wo = c.dram("ffn1_wo", [DFF, D], F32, "ExternalInput")
        mnw1 = c.dram("mix_nw", [D], F32, "ExternalInput")
        wlat = c.dram("w_lat", [D, 1088], F32, "ExternalInput")
        qn = c.dram("qn_w", [512], F32, "ExternalInput")
        kvn = c.dram("kvn_w", [512], F32, "ExternalInput")
        rope = c.dram("rope_cs", [TOK, 64], F32, "ExternalInput")
        x1 = c.dram("x1", [TOK, D], F32, "ExternalOutput")
        hT = c.dram("hT", [D, TOK], BF16, "ExternalOutput")
        latT = c.dram("latT", [1088, TOK], BF16, "ExternalOutput")
        ffn_block(c, K, cur, x1, nw, wi, wo, TOK)
        latents_block(c, K, x1, hT, latT, mnw1, wlat, qn, kvn, rope, TOK)
    c.finish()
    c.emit()
    return c


def kernel(x, ffn1_norm, ffn1_wi, ffn1_wo, mix_norm, w_in, mla_q_norm, mla_w_uq, mla_kv_norm, mla_w_ukv,
           hgrn_lb_logits, hgrn_norm, ssm_conv_w, ssm_conv_b, ssm_a_log, ssm_dt_bias, ssm_d, ssm_norm,
           w_o_mla, w_o_hgrn, w_o_ssm, w_out, ffn2_norm, ffn2_wi, ffn2_wo, final_norm):
    inp = dict(w_in=w_in, ssm_conv_w=ssm_conv_w, ssm_conv_b=ssm_conv_b, ssm_a_log=ssm_a_log,
               ssm_dt_bias=ssm_dt_bias, ssm_d=ssm_d, ssm_norm=ssm_norm)
    inp = {k: np.asarray(v, dtype=np.float32) for k, v in inp.items()}
    f32 = lambda a: np.ascontiguousarray(np.asarray(a, dtype=np.float32))
    S = x.shape[1]
    TOK = S // NCORES
    L = 2
    cores = list(range(NCORES))
    ident = np.eye(128, dtype=np.float32)
    rope = rope_table(S)
    hcs, mcs = hgrn_consts(), mamba_consts()
    xs = [f32(x[0, i * TOK:(i + 1) * TOK]) for i in cores]
    mixT_full = None
    x1 = None
    for l in range(L + 1):
        has_merge = l > 0
        has_p1 = l < L
        last = (l == L)
        ct = _get(("tok", TOK, has_merge, has_p1, last), lambda: build_tok(TOK, has_merge, has_p1, last))
        maps = []
        for i in cores:
            d = dict(ident=ident)
            if has_merge:
                lm = l - 1
                d.update(xin=x1[i], mixT=np.ascontiguousarray(mixT_full[:, i * TOK:(i + 1) * TOK]),
                         m_mix_nw=f32(mix_norm[lm]), wg=f32(w_in[lm][:, O_GATE:]), woa=f32(w_o_mla[lm]),
                         wob=f32(w_o_hgrn[lm]), woc=f32(w_o_ssm[lm]), wout=f32(w_out[lm]),
                         ffn2_nw=f32(ffn2_norm[lm]), ffn2_wi=f32(ffn2_wi[lm]), ffn2_wo=f32(ffn2_wo[lm]))
                if last:
                    d["final_nw"] = f32(final_norm)
            else:
                d["xin"] = xs[i]
            if has_p1:
                d.update(ffn1_nw=f32(ffn1_norm[l]), ffn1_wi=f32(ffn1_wi[l]), ffn1_wo=f32(ffn1_wo[l]),
                         mix_nw=f32(mix_norm[l]), w_lat=f32(w_in[l][:, :1088]), qn_w=f32(mla_q_norm[l]),
                         kvn_w=f32(mla_kv_norm[l]), rope_cs=rope[i * TOK:(i + 1) * TOK])
            maps.append(d)
        r1 = run_bass_kernel_spmd(ct.nc, maps, core_ids=cores).results
        if last:
            return np.concatenate([np.asarray(r["out"]) for r in r1], axis=0)[None].astype(np.float32)
        x1 = [np.asarray(r["x1"]) for r in r1]
        hT_all = np.ascontiguousarray(np.concatenate([np.asarray(r["hT"]) for r in r1], axis=1))
        latT_all = np.ascontiguousarray(np.concatenate([np.asarray(r["latT"]) for r in r1], axis=1))
        c2 = _get(("p2", S, l), lambda: build_p2(S, l))
        maps = []
        for i in cores:
            wq, wkv = mla_weights(f32(mla_w_uq[l]), f32(mla_w_ukv[l]), i)
            d = mamba_inputs(inp, l, i)
            d.update(hT=hT_all, latT=latT_all, wq=wq, wkv=wkv, rope_cs=rope, wh=hgrn_weights(inp["w_in"][l], i),
                     lg=f32(np.asarray(hgrn_lb_logits)[:, i * 256:(i + 1) * 256]),
                     hnw=f32(np.asarray(hgrn_norm)[l][i * 256:(i + 1) * 256]), hc=hcs, mc=mcs, ident=ident)
            maps.append(d)
        r2 = run_bass_kernel_spmd(c2.nc, maps, core_ids=cores).results
        mix = [np.asarray(r["mixT"]) for r in r2]
        mixT_full = np.concatenate([m[0:256] for m in mix] + [m[256:512] for m in mix] + [m[512:1024] for m in mix], axis=0)
```

```python
import contextlib
import numpy as np
import ml_dtypes
import concourse.bass as bass
import concourse.mybir as mybir
from concourse.bass_utils import run_bass_kernel_spmd

F32 = mybir.dt.float32
BF16 = mybir.dt.bfloat16
AF = mybir.ActivationFunctionType
ALU = mybir.AluOpType
AX = mybir.AxisListType

SAME_ENG_SYNC = "auto"
NDMA_SEMS = 36
NCORES = 8

D = 2048
DFF = 5632
KC = D // 128
FC = DFF // 128
EPS = 1e-6
SEQ = 8192

O_QLAT, O_KVLAT, O_KPE = 0, 512, 1024
O_HQ, O_HF, O_HI, O_HG = 1088, 1088 + 2048, 1088 + 4096, 1088 + 6144
O_Z = 1088 + 8192
O_XBC = O_Z + 4096
O_DT = O_XBC + 6144
O_GATE = O_DT + 64
IN_DIM = O_GATE + 3 * D


class Buf:
    __slots__ = ("name", "w", "r", "excl")

    def __init__(self, name, excl=False):
        self.name = name
        self.w = None
        self.r = []
        self.excl = excl


class V:
    __slots__ = ("ap", "bufs")

    def __init__(self, ap, bufs):
        self.ap = ap
        self.bufs = bufs

    def __getitem__(self, k):
        return V(self.ap[k], self.bufs)

    def sub(self, name, k=None):
        ap = self.ap if k is None else self.ap[k]
        return V(ap, [Buf(name)])

    def wap(self, ap):
        return V(ap, self.bufs)

    def re(self, s, **kw):
        return V(self.ap.rearrange(s, **kw), self.bufs)

    def bc(self, shape):
        return V(self.ap.broadcast_to(list(shape)), self.bufs)


class Op:
    __slots__ = ("eng", "fn", "deps", "dma", "slot", "same")

    def __init__(self, eng, fn, deps, dma, slot, same=True):
        self.same = same
        self.eng = eng
        self.fn = fn
        self.deps = deps
        self.dma = dma
        self.slot = slot


ENGS = ("pe", "act", "dve", "pool", "sp")


class Ctx:
    ARENA_BYTES = 204 * 1024

    def __init__(self):
        self.nc = bass.Bass("TRN2", target_bir_lowering=False)
        self.es = contextlib.ExitStack()
        self.ops = []
        self.rr = 0
        self.rrq = [0, 0, 0]
        self.same = True
        self.coll_ops = []
        self.slot_last = [None] * NDMA_SEMS
        self.last_op = {}
        self.fence_deps = {}
        self.arena = self.nc.alloc_sbuf_tensor("arena", [128, self.ARENA_BYTES // 4], F32)[:]
        self.bump = 0
        self.allocs = []
        self.marks = []
        self.banks = []
        for i in range(8):
            t = self.nc.alloc_psum_tensor("bank%d" % i, [128, 512], F32)
            self.banks.append(V(t[:], [Buf("bank%d" % i, excl=True)]))

    def dram(self, name, shape, dtype, kind="Internal"):
        t = self.nc.dram_tensor(name, list(shape), dtype, kind=kind)
        return V(t.ap(), [Buf(name)])

    def sbuf(self, name, shape, dtype):
        esz = 2 if dtype == BF16 else 4
        n = 1
        for s in shape[1:]:
            n *= s
        nbytes = (n * esz + 31) // 32 * 32
        assert self.bump + nbytes <= self.ARENA_BYTES, ("SBUF overflow", name, self.bump, nbytes)
        ap = self.arena[0:shape[0], self.bump // 4:(self.bump + nbytes) // 4]
        lo, hi = self.bump, self.bump + nbytes
        nb = Buf(name)
        olds = [b for (s0, e0, b) in self.allocs if s0 < hi and lo < e0]
        self.allocs.append((lo, hi, nb))
        self.bump += nbytes
        if dtype == BF16:
            ap = ap.bitcast(BF16)
        ap = ap[:, 0:n]
        if len(shape) == 3:
            ap = ap.rearrange("p (a b) -> p a b", a=shape[1])
        elif len(shape) == 4:
            ap = ap.rearrange("p (a b c) -> p a b c", a=shape[1], b=shape[2])
        return V(ap, [nb])

    def mark(self):
        self.marks.append(self.bump)

    def release(self):
        self.bump = self.marks.pop()
        self.fence()

    def bank(self, i, dtype=F32, shape=None, name=None):
        b = self.banks[i]
        ap = b.ap
        if dtype == BF16:
            ap = ap.bitcast(BF16)
        if shape is not None:
            n = 1
            for s in shape[1:]:
                n *= s
            ap = ap[0:shape[0], 0:n]
            if len(shape) == 3:
                ap = ap.rearrange("p (a b) -> p a b", a=shape[1])
        return V(ap, b.bufs if name is None else [Buf(name)])

    def fence(self):
        F = set(self.last_op.values())
        F.update(j for j in self.slot_last if j is not None)
        F.update(self.coll_ops)
        for e in ENGS:
            self.fence_deps.setdefault(e, set()).update(F)

    def rec(self, eng, fn, reads, writes, dma=False, coll=False):
        idx = len(self.ops)
        deps = set()
        fd = self.fence_deps.pop(eng, None)
        if fd:
            deps.update(fd)
        rb = [b for v in reads for b in v.bufs if not b.excl]
        wb = [b for v in writes for b in v.bufs] + [b for v in reads for b in v.bufs if b.excl]
        for b in rb:
            if b.w is not None:
                deps.add(b.w)
        for b in wb:
            if b.w is not None:
                deps.add(b.w)
            deps.update(b.r)
        for b in wb:
            b.w = idx
            b.r = []
        for b in rb:
            if True:
                if not dma:
                    b.r = [j for j in b.r if j != idx and (self.ops[j].dma or self.ops[j].eng != eng)]
                b.r.append(idx)
        slot = None
        if coll:
            slot = -1
            self.coll_ops.append(idx)
        elif dma:
            qi = {"pool": 0, "sp": 1, "act": 2}[eng]
            per = NDMA_SEMS // 3
            slot = qi * per + self.rrq[qi] % per
            self.rrq[qi] += 1
            if self.slot_last[slot] is not None:
                deps.add(self.slot_last[slot])
            self.slot_last[slot] = idx
        else:
            self.last_op[eng] = idx
        deps.discard(idx)
        self.ops.append(Op(eng, fn, deps, dma, slot, self.same))
        return idx

    def dma(self, out, in_, q="sp", slow=False):
        e = {"sp": self.nc.sync, "pool": self.nc.gpsimd, "act": self.nc.scalar}[q]
        if slow:
            fn = lambda: e.dma_start(out=out.ap, in_=in_.ap, allow_slow_non_contiguous=True)
        else:
            fn = lambda: e.dma_start(out=out.ap, in_=in_.ap)
        self.rec(q, fn, [in_], [out], dma=True)

    def mm(self, out, lhsT, rhs, start=True, stop=True):
        nc = self.nc
        self.rec("pe", lambda: nc.tensor.matmul(out.ap, lhsT.ap, rhs.ap, start=start, stop=stop),
                 [lhsT, rhs] + ([] if start else [out]), [out])

    def transpose(self, out, in_, ident):
        nc = self.nc
        self.rec("pe", lambda: nc.tensor.transpose(out.ap, in_.ap, ident.ap), [in_, ident], [out])

    def act(self, out, in_, func, bias=None, scale=None, accum_out=None):
        nc = self.nc
        reads = [in_]
        kw = {}
        if bias is not None:
            if isinstance(bias, V):
                reads.append(bias)
                kw["bias"] = bias.ap
            else:
                kw["bias"] = float(bias)
        if scale is not None:
            if isinstance(scale, V):
                reads.append(scale)
                kw["scale"] = scale.ap
            else:
                kw["scale"] = float(scale)
        writes = [out]
        if accum_out is not None:
            kw["accum_out"] = accum_out.ap
            writes.append(accum_out)
        self.rec("act", lambda: nc.scalar.activation(out=out.ap, in_=in_.ap, func=func, **kw), reads, writes)

    def _ve(self, eng):
        return {"dve": self.nc.vector, "pool": self.nc.gpsimd}[eng]

    def tt(self, out, in0, in1, op, eng="dve"):
        e = self._ve(eng)
        self.rec(eng, lambda: e.tensor_tensor(out=out.ap, in0=in0.ap, in1=in1.ap, op=op), [in0, in1], [out])

    def ts(self, out, in0, s1, op0, s2=None, op1=None, eng="dve", accum_out=None):
        e = self._ve(eng)
        reads = [in0]
        a1 = s1
        a2 = s2
        if isinstance(s1, V):
            reads.append(s1)
            a1 = s1.ap
        if isinstance(s2, V):
            reads.append(s2)
            a2 = s2.ap
        writes = [out]
        kw = {}
        if op1 is not None:
            kw["op1"] = op1
        if accum_out is not None:
            kw["accum_out"] = accum_out.ap
            writes.append(accum_out)
        self.rec(eng, lambda: e.tensor_scalar(out=out.ap, in0=in0.ap, scalar1=a1, scalar2=a2, op0=op0, **kw),
                 reads, writes)

    def stt(self, out, in0, scalar, in1, op0, op1):
        nc = self.nc
        reads = [in0, in1]
        a = scalar
        if isinstance(scalar, V):
            reads.append(scalar)
            a = scalar.ap
        self.rec("dve", lambda: nc.vector.scalar_tensor_tensor(out=out.ap, in0=in0.ap, scalar=a, in1=in1.ap,
                                                                op0=op0, op1=op1), reads, [out])

    def copy(self, out, in_, eng="dve"):
        if eng == "act":
            nc = self.nc
            self.rec("act", lambda: nc.scalar.copy(out=out.ap, in_=in_.ap), [in_], [out])
        else:
            e = self._ve(eng)
            self.rec(eng, lambda: e.tensor_copy(out=out.ap, in_=in_.ap), [in_], [out])

    def memset(self, out, val, eng="dve"):
        e = self._ve(eng)
        self.rec(eng, lambda: e.memset(out.ap, val), [], [out])

    def recip(self, out, in_):
        nc = self.nc
        self.rec("dve", lambda: nc.vector.reciprocal(out=out.ap, in_=in_.ap), [in_], [out])

    def reduce(self, out, in_, op=ALU.add, axis=AX.X):
        nc = self.nc
        self.rec("dve", lambda: nc.vector.tensor_reduce(out=out.ap, in_=in_.ap, axis=axis, op=op), [in_], [out])

    def finish(self):
        self.fence()
        self.rec("sp", lambda: None, [], [])

    def emit(self):
        nc = self.nc
        ops = self.ops
        n = len(ops)
        engs = {"pe": nc.tensor, "act": nc.scalar, "dve": nc.vector, "pool": nc.gpsimd, "sp": nc.sync}
        esem = {k: self.es.enter_context(nc.semaphore("s_" + k)) for k in engs}
        dsem = [self.es.enter_context(nc.semaphore("d%d" % i)) for i in range(NDMA_SEMS)]
        dval = [0] * NDMA_SEMS

        def needs(d, i):
            a, b = ops[d], ops[i]
            if a.dma:
                return True
            if a.eng == "sp":
                return False
            if a.eng == b.eng and not b.dma:
                if a.eng == "pe":
                    return False
                if b.same == "auto":
                    return a.eng in ("act", "pool")
                return b.same
            return True

        sig = [False] * n
        for i, op in enumerate(ops):
            for d in op.deps:
                if needs(d, i):
                    sig[d] = True
        cnt = {k: 0 for k in engs}
        val = [None] * n
        seen = {k: {} for k in engs}
        nwait = 0
        for i, op in enumerate(ops):
            E = op.eng
            waits = {}
            for d in op.deps:
                if needs(d, i):
                    key, s, v = val[d]
                    if key not in waits or waits[key][1] < v:
                        waits[key] = (s, v)
            sd = seen[E]
            for key, (s, v) in waits.items():
                if sd.get(key, 0) < v:
                    engs[E].wait_ge(s, v)
                    sd[key] = v
                    nwait += 1
            ins = op.fn()
            if ins is None:
                continue
            if op.dma and op.slot == -1:
                cs = self.es.enter_context(nc.semaphore("cc%d" % i))
                val[i] = ("cc%d" % i, cs, 1)
                ins.then_inc(cs)
            elif op.dma:
                k = op.slot
                dval[k] += 16
                val[i] = ("d%d" % k, dsem[k], dval[k])
                ins.then_inc(dsem[k], 16)
            elif sig[i]:
                cnt[E] += 1
                val[i] = (E, esem[E], cnt[E])
                ins.then_inc(esem[E], 1)
        self.stats = dict(n_ops=n, n_wait=nwait, n_sig=sum(sig))
        return nc


def load_cols(c, name, vec, n):
    t = c.sbuf(name, [128, n], F32)
    c.dma(t, vec.re("(j p) -> p j", p=128), slow=True)
    return t


def norm_transpose(c, K, x_src, nT, hT, nwc, Dm=D, eps=EPS, src_sbuf=None, tag="nt"):
    kc_n = Dm // 128
    xts = [c.sbuf(tag + "_x%d" % i, [128, Dm], F32) for i in range(2)] if src_sbuf is None else None
    junk = c.sbuf(tag + "_junk", [128, Dm], F32)
    xns = [c.sbuf(tag + "_xn%d" % i, [128, Dm], BF16) for i in range(2)]
    st = [c.sbuf(tag + "_st%d" % i, [128, 4], F32) for i in range(2)]
    for t in range(nT):
        if src_sbuf is None:
            xt = xts[t % 2]
            c.dma(xt, x_src[t * 128:(t + 1) * 128, :])
        else:
            xt = src_sbuf[t]
        s = st[t % 2]
        xn = xns[t % 2]
        c.act(junk, xt, AF.Square)
        c.reduce(s[:, 0:1], junk)
        c.act(s[:, 1:2], s[:, 0:1], AF.Sqrt, scale=1.0 / Dm, bias=K["eps"])
        c.recip(s[:, 2:3], s[:, 1:2])
        c.ts(xn, xt, s[:, 2:3], ALU.mult)
        for g in range((kc_n + 7) // 8):
            ng = min(8, kc_n - g * 8)
            pt = c.bank(K["tb"][(t * 2 + g) % 2], BF16, [128, 8, 128])
            for j in range(ng):
                kc = g * 8 + j
                c.transpose(pt[:, j, :], xn[:, kc * 128:(kc + 1) * 128], K["ident"])
            c.tt(hT[:, g * 8:g * 8 + ng, t * 128:(t + 1) * 128], pt[:, 0:ng, :],
                 nwc[:, g * 8:g * 8 + ng].wap(nwc.ap[:, g * 8:g * 8 + ng].unsqueeze(2).broadcast_to([128, ng, 128])),
                 ALU.mult)


def setup_consts(c, ident_d):
    K = {}
    K["ident"] = c.sbuf("ident", [128, 128], BF16)
    c.dma(K["ident"], ident_d, q="pool")
    K["eps"] = c.sbuf("epsc", [128, 1], F32)
    c.memset(K["eps"], EPS)
    K["tb"] = (6, 7)
    return K


def ffn_block(c, K, x_src, x_dst, nw, wi, wo, TOK):
    nT = TOK // 128
    TG = min(512, TOK)
    nG = TOK // TG
    c.mark()
    aT = c.sbuf("f_aT", [128, FC, TOK], BF16)
    c.mark()
    hT = c.sbuf("f_hT", [128, KC, TOK], BF16)
    nwc = load_cols(c, "f_nwc", nw, KC)
    c.mark()
    norm_transpose(c, K, x_src, nT, hT, nwc, tag="f_nt")
    c.release()
    WB = 256
    NWB = 3
    nblk = DFF // WB
    wg = [c.sbuf("f_wg%d" % i, [128, KC, WB], BF16) for i in range(NWB)]
    wu = [c.sbuf("f_wu%d" % i, [128, KC, WB], BF16) for i in range(NWB)]
    sg = [c.sbuf("f_sg%d" % i, [128, TG], F32) for i in range(2)]
    it = 0
    for b in range(nblk):
        g_t, u_t = wg[b % NWB], wu[b % NWB]
        c.dma(g_t, wi[:, b * WB:(b + 1) * WB].re("(kc p) n -> p kc n", p=128), q="pool")
        c.dma(u_t, wi[:, DFF + b * WB:DFF + (b + 1) * WB].re("(kc p) n -> p kc n", p=128), q="pool")
        for j in range(WB // 128):
            fc = b * (WB // 128) + j
            for tg in range(nG):
                pg = c.bank((it * 2) % 6, F32, [128, TG])
                pu = c.bank((it * 2 + 1) % 6, F32, [128, TG])
                for kc in range(KC):
                    c.mm(pg, g_t[:, kc, j * 128:(j + 1) * 128], hT[:, kc, tg * TG:(tg + 1) * TG],
                         start=(kc == 0), stop=(kc == KC - 1))
                for kc in range(KC):
                    c.mm(pu, u_t[:, kc, j * 128:(j + 1) * 128], hT[:, kc, tg * TG:(tg + 1) * TG],
                         start=(kc == 0), stop=(kc == KC - 1))
                s = sg[it % 2]
                c.act(s, pg, AF.Silu)
                c.tt(aT[:, fc, tg * TG:(tg + 1) * TG], pu, s, ALU.mult)
                it += 1
    c.release()
    c.mark()
    FG = 4
    wob = [c.sbuf("f_wo%d" % i, [128, FG, 512], BF16) for i in range(4)]
    xres = [c.sbuf("f_xr%d" % i, [128, 512], F32) for i in range(16)]
    n_pass_t = (nT + 7) // 8
    wi_ = 0
    pi = 0
    for tp in range(n_pass_t):
        tts = list(range(tp * 8, min(nT, tp * 8 + 8)))
        for dc in range(D // 512):
            xrs = xres[(pi % 2) * 8:(pi % 2) * 8 + 8]
            pi += 1
            for ti, t in enumerate(tts):
                c.dma(xrs[ti], x_src[t * 128:(t + 1) * 128, dc * 512:(dc + 1) * 512])
            for fg in range(FC // FG):
                w_t = wob[wi_ % 4]
                wi_ += 1
                c.dma(w_t, wo[fg * FG * 128:(fg + 1) * FG * 128, dc * 512:(dc + 1) * 512].re("(f p) n -> p f n", p=128),
                      q="pool")
                for ti, t in enumerate(tts):
                    for f in range(FG):
                        fc = fg * FG + f
                        c.mm(c.bank(ti), aT[:, fc, t * 128:(t + 1) * 128], w_t[:, f, :],
                             start=(fc == 0), stop=(fc == FC - 1))
            for ti, t in enumerate(tts):
                xr = xrs[ti]
                c.stt(xr, c.bank(ti), 0.5, xr, ALU.mult, ALU.add)
                c.dma(x_dst[t * 128:(t + 1) * 128, dc * 512:(dc + 1) * 512], xr)
    c.release()
    c.release()


def final_norm_block(c, K, x_src, out, nw, TOK):
    nT = TOK // 128
    c.mark()
    wbc = c.sbuf("fn_w", [128, D], F32)
    c.dma(wbc, nw.wap(nw.ap.partition_broadcast(128)))
    xt = [c.sbuf("fn_x%d" % i, [128, D], F32) for i in range(2)]
    junk = c.sbuf("fn_junk", [128, D], F32)
    st = [c.sbuf("fn_st%d" % i, [128, 4], F32) for i in range(2)]
    for t in range(nT):
        x = xt[t % 2]
        s = st[t % 2]
        c.dma(x, x_src[t * 128:(t + 1) * 128, :])
        c.act(junk, x, AF.Square)
        c.reduce(s[:, 0:1], junk)
        c.act(s[:, 1:2], s[:, 0:1], AF.Sqrt, scale=1.0 / D, bias=K["eps"])
        c.recip(s[:, 2:3], s[:, 1:2])
        c.stt(x, x, s[:, 2:3], wbc, ALU.mult, ALU.mult)
        c.dma(out[t * 128:(t + 1) * 128, :], x)
    c.release()


def latents_block(c, K, x1, hT_out, latT_out, mix_nw, w_in, qn_w, kvn_w, rope_cs, TOK):
    nT = TOK // 128
    c.mark()
    hT = c.sbuf("l_hT", [128, KC, TOK], BF16)
    nwc = load_cols(c, "l_nwc", mix_nw, KC)
    qnc = load_cols(c, "l_qnc", qn_w, 4)
    kvnc = load_cols(c, "l_kvnc", kvn_w, 4)
    wl = c.sbuf("l_wl", [128, KC, 1088], BF16)
    c.dma(wl[:, :, 0:512], w_in[:, 0:512].re("(kc p) n -> p kc n", p=128), q="pool")
    c.dma(wl[:, :, 512:1024], w_in[:, 512:1024].re("(kc p) n -> p kc n", p=128), q="pool")
    c.dma(wl[:, :, 1024:1088], w_in[:, 1024:1088].re("(kc p) n -> p kc n", p=128), q="pool")
    c.mark()
    norm_transpose(c, K, x1, nT, hT, nwc, tag="l_nt")
    c.release()
    c.dma(hT_out.re("(kc p) t -> p kc t", p=128), hT)
    latT = c.sbuf("l_latT", [128, 9, TOK], BF16)
    c.mark()
    cs = [c.sbuf("l_cs%d" % i, [128, 64], F32) for i in range(2)]
    kr = [c.sbuf("l_kr%d" % i, [128, 128], BF16) for i in range(2)]
    tmp = [c.sbuf("l_tmp%d" % i, [128, 4, 32], F32) for i in range(2)]
    for i in range(2):
        c.memset(kr[i], 0.0)
    junk = c.sbuf("l_junk", [128, 512], F32)
    xns = [c.sbuf("l_xn%d" % i, [128, 512], BF16) for i in range(2)]
    sts = [c.sbuf("l_st%d" % i, [128, 4], F32) for i in range(4)]
    for t in range(nT):
        pq = c.bank(0)
        pkv = c.bank(1)
        pk = c.bank(2, F32, [128, 64])
        for kc in range(KC):
            c.mm(pq, hT[:, kc, t * 128:(t + 1) * 128], wl[:, kc, 0:512], start=(kc == 0), stop=(kc == KC - 1))
        for kc in range(KC):
            c.mm(pkv, hT[:, kc, t * 128:(t + 1) * 128], wl[:, kc, 512:1024], start=(kc == 0), stop=(kc == KC - 1))
        for kc in range(KC):
            c.mm(pk, hT[:, kc, t * 128:(t + 1) * 128], wl[:, kc, 1024:1088], start=(kc == 0), stop=(kc == KC - 1))
        for li, (src, wcol) in enumerate(((pq, qnc), (pkv, kvnc))):
            s = sts[(t * 2 + li) % 4]
            xn = xns[li]
            c.act(junk, src, AF.Square)
            c.reduce(s[:, 0:1], junk)
            c.act(s[:, 1:2], s[:, 0:1], AF.Sqrt, scale=1.0 / 512, bias=K["eps"])
            c.recip(s[:, 2:3], s[:, 1:2])
            c.ts(xn, src, s[:, 2:3], ALU.mult)
            pt = c.bank(K["tb"][li], BF16, [128, 8, 128])
            for j in range(4):
                c.transpose(pt[:, j, :], xn[:, j * 128:(j + 1) * 128], K["ident"])
            c.tt(latT[:, li * 4:li * 4 + 4, t * 128:(t + 1) * 128], pt[:, 0:4, :],
                 wcol.wap(wcol.ap.unsqueeze(2).broadcast_to([128, 4, 128])), ALU.mult)
        csb = cs[t % 2]
        c.dma(csb, rope_cs[t * 128:(t + 1) * 128, :])
        tm = tmp[t % 2]
        krt = kr[t % 2]
        c.tt(tm[:, 0, :], pk[:, 0:32], csb[:, 0:32], ALU.mult)
        c.tt(tm[:, 1, :], pk[:, 32:64], csb[:, 32:64], ALU.mult)
        c.tt(tm[:, 2, :], pk[:, 0:32], csb[:, 32:64], ALU.mult)
        c.tt(tm[:, 3, :], pk[:, 32:64], csb[:, 0:32], ALU.mult)
        c.tt(krt[:, 0:32], tm[:, 0, :], tm[:, 1, :], ALU.subtract)
        c.tt(krt[:, 32:64], tm[:, 2, :], tm[:, 3, :], ALU.add)
        pt = c.bank(3, BF16, [128, 128])
        c.transpose(pt, krt, K["ident"])
        c.copy(latT[0:64, 8, t * 128:(t + 1) * 128], pt[0:64, :])
    c.release()
    c.dma(latT_out[0:1024, :].re("(j p) t -> p j t", p=128), latT[:, 0:8, :])
    c.dma(latT_out[1024:1088, :], latT[0:64, 8, :])
    c.release()


def rope_table(S):
    inv = 1.0 / (10000.0 ** (np.arange(0, 64, 2, dtype=np.float32) / np.float32(64)))
    ang = np.arange(S, dtype=np.float32)[:, None] * inv[None, :].astype(np.float32)
    return np.concatenate([np.cos(ang), np.sin(ang)], axis=1).astype(np.float32)


def bf16(a):
    return np.asarray(a).astype(ml_dtypes.bfloat16)


def build_p1(TOK):
    c = Ctx()
    x = c.dram("x", [TOK, D], F32, "ExternalInput")
    nw = c.dram("ffn_nw", [D], F32, "ExternalInput")
    wi = c.dram("ffn_wi", [D, 2 * DFF], F32, "ExternalInput")
    wo = c.dram("ffn_wo", [DFF, D], F32, "ExternalInput")
    mnw = c.dram("mix_nw", [D], F32, "ExternalInput")
    wlat = c.dram("w_lat", [D, 1088], F32, "ExternalInput")
    qn = c.dram("qn_w", [512], F32, "ExternalInput")
    kvn = c.dram("kvn_w", [512], F32, "ExternalInput")
    rope = c.dram("rope_cs", [TOK, 64], F32, "ExternalInput")
    ident = c.dram("ident", [128, 128], F32, "ExternalInput")
    x1 = c.dram("x1", [TOK, D], F32, "ExternalOutput")
    hT = c.dram("hT", [D, TOK], BF16, "ExternalOutput")
    latT = c.dram("latT", [1088, TOK], BF16, "ExternalOutput")
    K = setup_consts(c, ident)
    ffn_block(c, K, x, x1, nw, wi, wo, TOK)
    latents_block(c, K, x1, hT, latT, mnw, wlat, qn, kvn, rope, TOK)
    c.finish()
    c.emit()
    return c


def mla_block(c, K, lat_src, kr_src, wq_d, wkv_d, rope_cs, out_dst, S):
    TQ = min(512, S)
    nQB = S // TQ
    nTT = TQ // 128
    scale = 192.0 ** -0.5
    c.same = "auto"
    c.mark()
    knT = c.sbuf("m_knT", [128, 2, S], BF16)
    Vt = c.sbuf("m_Vt", [128, S // 128, 256], BF16)
    krA = c.sbuf("m_krA", [128, S], BF16)
    krB = c.sbuf("m_krB", [128, S], BF16)
    c.memset(krA[64:128, :], 0.0)
    c.memset(krB[0:64, :], 0.0)
    ones = c.sbuf("m_ones", [128, 128], BF16)
    c.memset(ones, 1.0)
    wq = c.sbuf("m_wq", [128, 4, 384], BF16)
    wkv = c.sbuf("m_wkv", [128, 4, 512], BF16)
    c.dma(wq, wq_d.re("(j p) n -> p j n", p=128), q="pool")
    c.dma(wkv, wkv_d.re("(j p) n -> p j n", p=128), q="pool")
    lat = [c.sbuf("m_lat%d" % i, [128, 8, TQ], BF16) for i in range(2)]
    qnT = [c.sbuf("m_qnT%d" % i, [128, 2, TQ], BF16) for i in range(2)]
    qpT = [c.sbuf("m_qpT%d" % i, [128, TQ], BF16) for i in range(2)]
    cs = [c.sbuf("m_cs%d" % i, [128, 64], F32) for i in range(2)]
    tmp = [c.sbuf("m_tmp%d" % i, [128, 4, 2, 32], F32) for i in range(2)]
    qpe = [c.sbuf("m_qpe%d" % i, [128, 2, 2, 32], BF16) for i in range(2)]
    PT = [c.sbuf("m_PT%d" % i, [128, TQ], BF16) for i in range(4)]
    rl = [c.sbuf("m_rl%d" % i, [128, TQ], F32) for i in range(2)]
    oT = [c.sbuf("m_oT%d" % i, [128, TQ], BF16) for i in range(2)]
    rb = [0]

    def rbank(dtype=F32, shape=None):
        b = c.bank(rb[0] % 4, dtype, shape)
        rb[0] += 1
        return b

    blk = [0]

    def prologue(qb):
        t0 = qb * TQ
        lt = lat[qb % 2]
        c.dma(lt, lat_src(t0, TQ).re("(j p) t -> p j t", p=128))
        c.dma(krA[0:64, t0:t0 + TQ], kr_src(t0, TQ))
        c.dma(krB[64:128, t0:t0 + TQ], kr_src(t0, TQ))
        qn = qnT[qb % 2]
        qp = qpT[qb % 2]
        for h in range(2):
            pk = rbank(F32, [128, TQ])
            for j in range(4):
                c.mm(pk, wkv[:, j, h * 128:(h + 1) * 128], lt[:, 4 + j, :], start=(j == 0), stop=(j == 3))
            c.copy(knT[:, h, t0:t0 + TQ], pk, eng="act")
            pq = rbank(F32, [128, TQ])
            for j in range(4):
                c.mm(pq, wq[:, j, h * 128:(h + 1) * 128], lt[:, j, :], start=(j == 0), stop=(j == 3))
            c.copy(qn[:, h, :], pq)
        for tt in range(nTT):
            gt = (t0 // 128) + tt
            pv = rbank(F32, [128, 256])
            for j in range(4):
                c.mm(pv, lt[:, 4 + j, tt * 128:(tt + 1) * 128], wkv[:, j, 256:512], start=(j == 0), stop=(j == 3))
            c.copy(Vt[:, gt, :], pv, eng="act")
            pr = rbank(F32, [128, 128])
            for j in range(4):
                c.mm(pr, lt[:, j, tt * 128:(tt + 1) * 128], wq[:, j, 256:384], start=(j == 0), stop=(j == 3))
            csb = cs[tt % 2]
            c.dma(csb, rope_cs[t0 + tt * 128:t0 + (tt + 1) * 128, :])
            tm = tmp[tt % 2]
            qe = qpe[tt % 2]
            prv = pr.wap(pr.ap.rearrange("p (h two i) -> p h two i", h=2, two=2))
            cosb = csb.wap(csb.ap[:, 0:32].unsqueeze(1).broadcast_to([128, 2, 32]))
            sinb = csb.wap(csb.ap[:, 32:64].unsqueeze(1).broadcast_to([128, 2, 32]))
            c.tt(tm[:, 0, :, :], prv[:, :, 0, :], cosb, ALU.mult)
            c.tt(tm[:, 1, :, :], prv[:, :, 1, :], sinb, ALU.mult)
            c.tt(tm[:, 2, :, :], prv[:, :, 0, :], sinb, ALU.mult)
            c.tt(tm[:, 3, :, :], prv[:, :, 1, :], cosb, ALU.mult)
            c.tt(qe[:, :, 0, :], tm[:, 0, :, :], tm[:, 1, :, :], ALU.subtract)
            c.tt(qe[:, :, 1, :], tm[:, 2, :, :], tm[:, 3, :, :], ALU.add)
            ptq = rbank(BF16, [128, 128])
            c.transpose(ptq, qe.re("p h two i -> p (h two i)"), K["ident"])
            c.copy(qp[:, tt * 128:(tt + 1) * 128], ptq)

    def flash(qb):
        t0 = qb * TQ
        qn = qnT[qb % 2]
        qp = qpT[qb % 2]
        for h in range(2):
            kr = krA if h == 0 else krB
            nkt = (t0 + TQ) // 128
            po = c.bank(4 + blk[0] % 2, F32, [128, TQ])
            pl = c.bank(6 + blk[0] % 2, F32, [128, TQ])

            def scores(kt):
                c0 = max(0, kt * 128 - t0)
                ps = rbank(F32, [128, TQ])
                c.mm(ps[:, c0:TQ], knT[:, h, kt * 128:(kt + 1) * 128], qn[:, h, c0:TQ], start=True, stop=False)
                c.mm(ps[:, c0:TQ], kr[:, kt * 128:(kt + 1) * 128], qp[:, c0:TQ], start=False, stop=True)
                return ps, c0

            nxt = scores(0)
            for kt in range(nkt):
                ps, c0 = nxt
                if kt + 1 < nkt:
                    nxt = scores(kt + 1)
                pt = PT[kt % 4]
                c.act(pt[:, c0:TQ], ps[:, c0:TQ], AF.Exp, scale=scale)
                if kt * 128 >= t0:
                    c.memset(pt[64:128, c0:c0 + 64], 0.0)
                c.mm(po[:, c0:TQ], Vt[:, kt, h * 128:(h + 1) * 128], pt[:, c0:TQ], start=(kt == 0), stop=(kt == nkt - 1))
                c.mm(pl[:, c0:TQ], ones, pt[:, c0:TQ], start=(kt == 0), stop=(kt == nkt - 1))
            r = rl[blk[0] % 2]
            o = oT[blk[0] % 2]
            c.recip(r, pl)
            c.tt(o, po, r, ALU.mult)
            c.dma(out_dst(h, t0, TQ), o)
            blk[0] += 1
    prologue(0)
    for qb in range(nQB):
        if qb + 1 < nQB:
            prologue(qb + 1)
        flash(qb)
    c.release()
    c.same = True


def build_p2_mla(S):
    c = Ctx()
    latT = c.dram("latT", [1088, S], BF16, "ExternalInput")
    wq = c.dram("wq", [512, 384], F32, "ExternalInput")
    wkv = c.dram("wkv", [512, 512], F32, "ExternalInput")
    rope = c.dram("rope_cs", [S, 64], F32, "ExternalInput")
    ident = c.dram("ident", [128, 128], F32, "ExternalInput")
    out = c.dram("mlaT", [256, S], BF16, "ExternalOutput")
    K = setup_consts(c, ident)
    mla_block(c, K, lambda t0, n: latT[0:1024, t0:t0 + n], lambda t0, n: latT[1024:1088, t0:t0 + n], wq, wkv, rope,
              lambda h, t0, n: out[h * 128:(h + 1) * 128, t0:t0 + n], S)
    c.finish()
    c.emit()
    return c


def mla_weights(w_uq, w_ukv, core):
    h0 = 2 * core
    q = w_uq.reshape(512, 16, 192)
    kv = w_ukv.reshape(512, 16, 256)
    wq = np.concatenate([q[:, h0, :128], q[:, h0 + 1, :128], q[:, h0, 128:], q[:, h0 + 1, 128:]], axis=1)
    wkv = np.concatenate([kv[:, h0, :128], kv[:, h0 + 1, :128], kv[:, h0, 128:], kv[:, h0 + 1, 128:]], axis=1)
    return np.ascontiguousarray(wq), np.ascontiguousarray(wkv)


def run_pipeline(starters, stagger, maxact=3):
    active = []
    nxt = 0
    since = stagger
    while active or nxt < len(starters):
        if nxt < len(starters) and since >= stagger and len(active) < maxact:
            active.append(starters[nxt]())
            nxt += 1
            since = 0
        still = []
        for gens in active:
            alive = []
            for g in gens:
                try:
                    next(g)
                    alive.append(g)
                except StopIteration:
                    pass
            if alive:
                still.append(alive)
        active = still
        since += 1
        if not active:
            since = stagger


HG_STAGGER = 6
MB_STAGGER = 4
MB_MAXACT = 3


def hgrn_consts():
    s = np.arange(128)
    same = (s[:, None] // 64) == (s[None, :] // 64)
    tri = (same & (s[:, None] <= s[None, :])).astype(np.float32)
    mref = (same & ((s[:, None] % 64) <= 31)).astype(np.float32)
    ones = same.astype(np.float32)
    cind = np.stack([(s < 64), (s >= 64)], axis=1).astype(np.float32)
    m = np.zeros((128, 512), np.float32)
    m[:, 0:128] = tri
    m[:, 128:256] = tri - mref
    m[:, 256:384] = ones - tri
    m[:, 384:386] = cind
    return m


def hgrn_block(c, K, hT_src, wh_d, lg_d, nw_d, hc_d, out_dst, S, layer):
    TB = min(512, S)
    nB = S // TB
    nTT = TB // 128
    c.same = "auto"
    c.mark()
    wh = c.sbuf("g_wh", [128, KC, 1024], BF16)
    for h in range(2):
        c.dma(wh[:, :, h * 512:(h + 1) * 512], wh_d[:, h * 512:(h + 1) * 512].re("(kc p) n -> p kc n", p=128), q="pool")
    hc = c.sbuf("g_hc", [128, 512], F32)
    c.dma(hc, hc_d)
    tri = hc[:, 0:128]
    lg = c.sbuf("g_lg", [128, 2, 256], F32)
    c.dma(lg, lg_d.wap(lg_d.ap.rearrange("l n -> (l n)").partition_broadcast(128)).re("p (l n) -> p l n", l=2))
    ee = c.sbuf("g_ee", [128, 2, 256], F32)
    c.act(ee, lg, AF.Exp)
    den = c.sbuf("g_den", [128, 256], F32)
    c.tt(den, ee[:, 0, :], ee[:, 1, :], ALU.add)
    c.recip(den, den)
    lb = c.sbuf("g_lb", [128, 256], F32)
    oml = c.sbuf("g_oml", [128, 256], F32)
    if layer == 0:
        c.ts(lb, ee[:, 0, :], 0.0, ALU.mult)
    else:
        c.tt(lb, ee[:, 1, :], den, ALU.mult)
    c.ts(oml, lb, -1.0, ALU.mult, 1.0, ALU.add)
    nwb = c.sbuf("g_nwb", [128, 256], F32)
    c.dma(nwb, nw_d.wap(nw_d.ap.partition_broadcast(128)))
    hTb = [c.sbuf("g_hT%d" % i, [128, KC, TB], BF16) for i in range(2)]
    ost = [[c.sbuf("g_ost%d_%d" % (h, i), [128, TB], BF16) for i in range(2)] for h in range(2)]

    class HS:
        pass

    states = []
    for h in range(2):
        stt_ = HS()
        stt_.S = [c.sbuf("g_S%d_%d" % (h, i), [128, 128], F32) for i in range(2)]
        stt_.Sb = [c.sbuf("g_Sb%d_%d" % (h, i), [128, 128], BF16) for i in range(2)]
        c.memset(stt_.S[0], 0.0)
        c.memset(stt_.Sb[0], 0.0)
        states.append(stt_)
    hs = {}
    for h in range(2):
      for par in range(2):
        o = HS()
        o.state = states[h]
        n = "g%d%d_" % (h, par)
        o.f = c.sbuf(n + "f", [128, 128], F32)
        o.E4 = c.sbuf(n + "E4", [128, 512], F32)
        o.osb = c.sbuf(n + "osb", [128, 128], F32)
        o.glog = c.sbuf(n + "glog", [128, 128], F32)
        o.kk = c.sbuf(n + "kk", [128, 128], F32)
        o.E3 = c.sbuf(n + "E3", [128, 3, 128], F32)
        o.ek = c.sbuf(n + "ek", [128, 128], F32)
        o.ebl = c.sbuf(n + "ebl", [128, 2], F32)
        o.qs = c.sbuf(n + "qs", [128, 128], F32)
        o.gw = c.sbuf(n + "gw", [128, 128], F32)
        o.tok = c.sbuf(n + "tok", [128, 3, 128], BF16)
        o.kdp = [c.sbuf(n + "kdp%d" % i, [128, 128], BF16) for i in range(2)]
        o.v = c.sbuf(n + "v", [128, 128], BF16)
        o.qT = c.sbuf(n + "qT", [128, 128], BF16)
        o.qbp = [c.sbuf(n + "qbp%d" % i, [128, 128], BF16) for i in range(2)]
        o.kTp = [c.sbuf(n + "kTp%d" % i, [128, 128], BF16) for i in range(2)]
        o.ATm = c.sbuf(n + "ATm", [128, 128], BF16)
        o.sq = c.sbuf(n + "sq", [128, 128], F32)
        o.st = c.sbuf(n + "st", [128, 4], F32)
        o.y = c.sbuf(n + "y", [128, 128], BF16)
        for t_ in o.kdp + o.qbp + o.kTp:
            c.memset(t_, 0.0)
        b0 = 4 * h
        o.proj = c.bank(b0)
        o.bb = c.bank(b0 + 1, F32, [128, 384])
        o.pb4 = c.bank(b0 + 1)
        o.pb4 = o.pb4.wap(o.pb4.ap[:, 384:386])
        b2 = c.banks[b0 + 2]
        o.ptr = V(b2.ap.bitcast(BF16)[:, 0:384].rearrange("p (a b) -> p a b", a=3), b2.bufs)
        o.pyT = V(b2.ap.bitcast(BF16)[:, 384:512], b2.bufs)
        o.pS = [V(b2.ap[:, 256:384], b2.bufs), V(b2.ap[:, 384:512], b2.bufs)]
        o.pAT = c.bank(b0 + 3, F32, [128, 128])
        b3 = c.banks[b0 + 3]
        o.po = V(b3.ap[:, 128:256], b3.bufs)
        hs[(h, par)] = o

    def tile_gen(h, hTt, tt, ostage, par, fin):
        o = hs[(h, par)]
        tsl = slice(tt * 128, (tt + 1) * 128)
        for kc in range(KC):
            c.mm(o.proj, hTt[:, kc, tsl], wh[:, kc, h * 512:(h + 1) * 512], start=(kc == 0), stop=(kc == KC - 1))
        q_in, f_in, i_in, g_in = (o.proj[:, j * 128:(j + 1) * 128] for j in range(4))
        hsl = slice(h * 128, (h + 1) * 128)
        yield
        c.act(o.E4, o.proj, AF.Exp, scale=-1.0)
        c.copy(o.v, i_in, eng="act")
        c.ts(o.E4, o.E4, 1.0, ALU.add)
        c.recip(o.E4, o.E4)
        c.tt(o.qs, q_in, o.E4[:, 0:128], ALU.mult)
        c.tt(o.gw, g_in, o.E4[:, 384:512], ALU.mult)
        c.tt(o.f, o.E4[:, 128:256], oml[:, hsl], ALU.mult)
        c.tt(o.f, o.f, lb[:, hsl], ALU.add)
        c.act(o.glog, o.f, AF.Ln)
        c.ts(o.kk, o.f, -1.0, ALU.mult, 1.0, ALU.add, eng="pool")
        c.tt(o.gw, o.gw, nwb[:, hsl], ALU.mult, eng="pool")
        yield
        for j in range(3):
            c.mm(o.bb[:, j * 128:(j + 1) * 128], hc[:, j * 128:(j + 1) * 128], o.glog)
        c.mm(o.pb4, o.glog, hc[:, 384:386])
        yield
        c.act(o.E3, o.bb.re("p (a b) -> p a b", a=3), AF.Exp)
        c.act(o.ek, o.bb[:, 128:256], AF.Exp, scale=-1.0)
        c.act(o.ebl, o.pb4, AF.Exp)
        sc = 128.0 ** -0.5
        c.stt(o.tok[:, 0, :], o.qs, sc, o.E3[:, 1, :], ALU.mult, ALU.mult)
        c.stt(o.tok[:, 1, :], o.qs, sc, o.E3[:, 0, :], ALU.mult, ALU.mult)
        c.tt(o.tok[:, 2, :], o.kk, o.ek, ALU.mult, eng="pool")
        c.tt(o.kdp[0][0:64, :], o.kk[0:64, :], o.E3[0:64, 2, :], ALU.mult, eng="pool")
        c.tt(o.kdp[1][64:128, :], o.kk[64:128, :], o.E3[64:128, 2, :], ALU.mult, eng="pool")
        yield
        for j in range(3):
            c.transpose(o.ptr[:, j, :], o.tok[:, j, :], K["ident"])
        c.copy(o.qT, o.ptr[:, 0, :], eng="act")
        c.copy(o.qbp[0][:, 0:64], o.ptr[:, 1, 0:64])
        c.copy(o.qbp[1][:, 64:128], o.ptr[:, 1, 64:128])
        c.copy(o.kTp[0][:, 0:64], o.ptr[:, 2, 0:64])
        c.copy(o.kTp[1][:, 64:128], o.ptr[:, 2, 64:128], eng="act")
        yield
        c.mm(o.pAT[:, 0:64], o.kTp[0], o.qT[:, 0:64])
        c.mm(o.pAT[:, 64:128], o.kTp[1], o.qT[:, 64:128])
        c.tt(o.ATm, o.pAT, tri, ALU.mult)
        S0, S1 = o.state.S[0], o.state.S[1]
        Sb0, Sb1 = o.state.Sb[0], o.state.Sb[1]
        c.mm(o.pS[0], o.kdp[0], o.v)
        c.stt(S1, S0, o.ebl[:, 0:1], o.pS[0], ALU.mult, ALU.add)
        c.copy(Sb1, S1, eng="act")
        yield
        c.mm(o.po, o.ATm, o.v, start=True, stop=False)
        c.mm(o.po, o.qbp[0], Sb0, start=False, stop=False)
        c.mm(o.po, o.qbp[1], Sb1, start=False, stop=True)
        c.mm(o.pS[1], o.kdp[1], o.v)
        c.stt(S0, S1, o.ebl[:, 1:2], o.pS[1], ALU.mult, ALU.add)
        c.copy(Sb0, S0, eng="act")
        yield
        c.copy(o.osb, o.po, eng="act")
        c.tt(o.sq, o.osb, o.osb, ALU.mult, eng="pool")
        c.reduce(o.st[:, 0:1], o.sq)
        c.act(o.st[:, 1:2], o.st[:, 0:1], AF.Ln, scale=1.0 / 128, bias=K["eps"])
        c.act(o.st[:, 2:3], o.st[:, 1:2], AF.Exp, scale=-0.5)
        c.stt(o.y, o.osb, o.st[:, 2:3], o.gw, ALU.mult, ALU.mult)
        yield
        c.transpose(o.pyT, o.y, K["ident"])
        c.copy(ostage[:, tsl], o.pyT)
        if fin is not None:
            fin()
        yield

    starters = []
    gi = 0
    for b in range(nB):
        for tt in range(nTT):
            def start(b=b, tt=tt, gi=gi):
                hTt = hTb[b % 2]
                if tt == 0:
                    c.dma(hTt, hT_src(b * TB, TB).re("(kc p) t -> p kc t", p=128))
                gl = []
                for h in range(2):
                    fin = None
                    if tt == nTT - 1:
                        fin = (lambda h=h, b=b: c.dma(out_dst(h, b * TB, TB), ost[h][b % 2]))
                    gl.append(tile_gen(h, hTt, tt, ost[h][b % 2], gi % 2, fin))
                return gl
            starters.append(start)
            gi += 1
    run_pipeline(starters, HG_STAGGER)
    c.release()
    c.same = True


def build_p2_hgrn(S, layer):
    c = Ctx()
    hT = c.dram("hT", [D, S], BF16, "ExternalInput")
    wh = c.dram("wh", [D, 1024], F32, "ExternalInput")
    lg = c.dram("lg", [2, 256], F32, "ExternalInput")
    nw = c.dram("hnw", [256], F32, "ExternalInput")
    hc = c.dram("hc", [128, 512], F32, "ExternalInput")
    ident = c.dram("ident", [128, 128], F32, "ExternalInput")
    out = c.dram("hgT", [256, S], BF16, "ExternalOutput")
    K = setup_consts(c, ident)
    hgrn_block(c, K, lambda t0, n: hT[:, t0:t0 + n], wh, lg, nw, hc, lambda h, t0, n: out[h * 128:(h + 1) * 128, t0:t0 + n], S, layer)
    c.finish()
    c.emit()
    return c


def hgrn_weights(w_in, core):
    cols = []
    for h in (2 * core, 2 * core + 1):
        for off in (O_HQ, O_HF, O_HI, O_HG):
            cols.append(w_in[:, off + h * 128: off + (h + 1) * 128])
    return np.ascontiguousarray(np.concatenate(cols, axis=1))


def mamba_consts():
    m = np.zeros((128, 512), np.float32)
    s = np.arange(128)
    same = (s[:, None] // 64) == (s[None, :] // 64)
    tri = (same & (s[:, None] <= s[None, :])).astype(np.float32)
    m[:, 0:128] = 1.0
    m[:, 128:256] = -tri
    m[:, 256:384] = (s[:, None] < 64) * 1.0 + 0.0 * s[None, :]
    m[:, 384:512] = (s[:, None] >= 64) * 1.0 + 0.0 * s[None, :]
    return m


def mamba_block(c, K, hT_src, wm_d, cw_d, cb_d, sp_d, nw_d, hc_d, mc_d, out_dst, S):
    TB = min(512, S)
    nB = S // TB
    nTT = TB // 128
    c.mark()
    wm = c.sbuf("s_wm", [128, KC, 1288], BF16)
    for (a, b) in ((0, 512), (512, 1024), (1024, 1288)):
        c.dma(wm[:, :, a:b], wm_d[:, a:b].re("(kc p) n -> p kc n", p=128), q="pool")
    hc = c.sbuf("s_hc", [128, 512], F32)
    c.dma(hc, hc_d)
    mc = c.sbuf("s_mc", [128, 512], F32)
    c.dma(mc, mc_d)
    tri = hc[:, 0:128]
    omt = hc[:, 256:384]
    ones_full = mc[:, 0:128]
    ntri = mc[:, 128:256]
    cw = c.sbuf("s_cw", [128, 6, 4], F32)
    for j in range(4):
        c.dma(cw[:, :, j], cw_d[j, :].re("(cc p) -> p cc", p=128), slow=True)
    cbias = load_cols(c, "s_cb", cb_d, 6)
    spb = c.sbuf("s_spb", [128, 3, 8], F32)
    c.dma(spb, sp_d.wap(sp_d.ap.rearrange("a h -> (a h)").partition_broadcast(128)).re("p (a h) -> p a h", a=3))
    abc = c.sbuf("s_abc", [128, 8], F32)
    c.act(abc, spb[:, 0, :], AF.Exp)
    c.ts(abc, abc, -1.0, ALU.mult)
    one = c.sbuf("s_one", [128, 1], F32)
    c.memset(one, 1.0)
    nwb = c.sbuf("s_nwb", [128, 512], F32)
    c.dma(nwb, nw_d.wap(nw_d.ap.partition_broadcast(128)))
    hTb = [c.sbuf("s_hT%d" % i, [128, KC, TB], BF16) for i in range(2)]
    xpre = c.sbuf("s_xpre", [128, 6, TB + 4], F32)
    c.memset(xpre, 0.0)
    cacc = [c.sbuf("s_cacc%d" % i, [128, TB], F32) for i in range(2)]
    xcTs = [c.sbuf("s_xcT%d" % i, [128, 6, TB], BF16) for i in range(2)]
    ost = [c.sbuf("s_ost%d" % i, [128, 4, TB], BF16) for i in range(2)]
    HT = c.sbuf("s_HT", [128, 8, 64], F32)
    HTb = [c.sbuf("s_HTb%d" % i, [128, 512], BF16) for i in range(2)]
    c.memset(HT, 0.0)
    c.memset(HTb[0], 0.0)

    class SC:
        pass

    scr = []
    for par in range(2):
        o = SC()
        n = "s%d_" % par
        o.xs_tm = c.sbuf(n + "xs", [128, 8, 64], BF16)
        o.Bp = [c.sbuf(n + "Bp%d" % i, [128, 128], BF16) for i in range(2)]
        o.CTp = [c.sbuf(n + "CTp%d" % i, [128, 128], BF16) for i in range(2)]
        for t_ in o.Bp + o.CTp:
            c.memset(t_, 0.0)
        o.sm = c.sbuf(n + "sm", [128, 8, 8], F32)
        o.dtA = c.sbuf(n + "dtA", [128, 8], F32)
        o.rhsR = c.sbuf(n + "rhsR", [128, 8, 128], F32)
        o.dtAb = c.sbuf(n + "dtAb", [128, 8, 128], F32)
        o.segm = c.sbuf(n + "segm", [128, 8, 128], F32)
        o.Dm = c.sbuf(n + "Dm", [128, 8, 128], F32)
        o.cbm = c.sbuf(n + "cbm", [128, 128], F32)
        o.MT = c.sbuf(n + "MT", [128, 8, 128], BF16)
        o.xdt = c.sbuf(n + "xdt", [128, 8, 64], BF16)
        o.xdd = c.sbuf(n + "xdd", [128, 8, 64], BF16)
        o.sz = c.sbuf(n + "sz", [128, 512], F32)
        o.y1 = c.sbuf(n + "y1", [128, 8, 64], F32)
        o.sq = c.sbuf(n + "sq", [128, 512], F32)
        o.st = c.sbuf(n + "st", [128, 4], F32)
        o.yb = c.sbuf(n + "yb", [128, 512], BF16)
        o.cd = c.sbuf(n + "cd", [128, 2, 8], F32)
        scr.append(o)

    bk_tr = c.banks[2]
    ptr = V(bk_tr.ap.bitcast(BF16)[:, 0:640].rearrange("p (a b) -> p a b", a=5), bk_tr.bufs)
    b3 = c.banks[3]

    def prologue(b):
        hTt = hTb[b % 2]
        xcT = xcTs[b % 2]
        c.dma(hTt, hT_src(b * TB, TB).re("(kc p) t -> p kc t", p=128))
        for cc in range(6):
            pp = c.bank(cc % 2, F32, [128, TB])
            for kc in range(KC):
                c.mm(pp, wm[:, kc, 512 + cc * 128:512 + (cc + 1) * 128], hTt[:, kc, :], start=(kc == 0), stop=(kc == KC - 1))
            c.copy(xpre[:, cc, 3:3 + TB], pp, eng="act")
            ac = cacc[cc % 2]
            c.ts(ac, xpre[:, cc, 0:TB], cw[:, cc, 0:1], ALU.mult)
            for j in (1, 2, 3):
                c.stt(ac, xpre[:, cc, j:j + TB], cw[:, cc, j:j + 1], ac, ALU.mult, ALU.add)
            c.act(xcT[:, cc, :], ac, AF.Silu, bias=cbias[:, cc:cc + 1])
            c.copy(xpre[:, cc, 0:3], xpre[:, cc, TB:TB + 3], eng="pool")

    def tile_gen(b, tt, par):
        o = scr[par]
        hTt = hTb[b % 2]
        xcT = xcTs[b % 2]
        sm, dtA = o.sm, o.dtA
        tsl = slice(tt * 128, (tt + 1) * 128)
        pz = c.bank(tt % 2, F32, [128, 512])
        for kc in range(KC):
            c.mm(pz, hTt[:, kc, tsl], wm[:, kc, 0:512], start=(kc == 0), stop=(kc == KC - 1))
        pdt = V(b3.ap[:, 0:8], b3.bufs)
        for kc in range(KC):
            c.mm(pdt, hTt[:, kc, tsl], wm[:, kc, 1280:1288], start=(kc == 0), stop=(kc == KC - 1))
        c.act(o.sz, pz, AF.Silu)
        yield
        c.tt(sm[:, 0, :], pdt, spb[:, 1, :], ALU.add)
        c.act(sm[:, 1, :], sm[:, 0, :], AF.Exp)
        c.act(sm[:, 2, :], sm[:, 1, :], AF.Ln, bias=one)
        c.tt(dtA, sm[:, 2, :], abc, ALU.mult)
        for j in range(5):
            c.transpose(ptr[:, j, :], xcT[:, j, tsl], K["ident"])
        c.copy(o.xs_tm.re("p h q -> p (h q)"), ptr[:, 0:4, :].re("p a b -> p (a b)"), eng="act")
        c.copy(o.Bp[0][0:64, :], ptr[0:64, 4, :])
        c.copy(o.Bp[1][64:128, :], ptr[64:128, 4, :])
        c.copy(o.CTp[0][:, 0:64], xcT[:, 5, tt * 128:tt * 128 + 64], eng="pool")
        c.copy(o.CTp[1][:, 64:128], xcT[:, 5, tt * 128 + 64:tt * 128 + 128], eng="pool")
        yield
        pac = V(b3.ap[:, 8:16], b3.bufs)
        plm = V(b3.ap[:, 16:24], b3.bufs)
        pcd0 = V(b3.ap[:, 24:32], b3.bufs)
        pcd1 = V(b3.ap[:, 32:40], b3.bufs)
        pcb = V(b3.ap[:, 128:256], b3.bufs)
        c.mm(pac, tri, dtA)
        c.mm(plm, omt, dtA)
        c.mm(pcd0, mc[:, 256:384], dtA)
        c.mm(pcd1, mc[:, 384:512], dtA)
        c.mm(pcb, xcT[:, 4, tsl], xcT[:, 5, tsl])
        c.act(sm[:, 3:7, :].re("p a h -> p (a h)"), V(b3.ap[:, 8:40], b3.bufs), AF.Exp)
        c.tt(o.cbm, pcb, tri, ALU.mult)
        dbc = dtA.wap(dtA.ap.unsqueeze(2).broadcast_to([128, 8, 128]))
        tbc = tri.wap(tri.ap.unsqueeze(1).broadcast_to([128, 8, 128]))
        c.tt(o.rhsR, dbc, tbc, ALU.mult)
        c.copy(o.dtAb, dbc, eng="pool")
        yield
        for half in range(2):
            pseg = c.bank(4 + half)
            c.mm(pseg, ones_full, o.rhsR[:, half * 4:(half + 1) * 4, :].re("p a b -> p (a b)"), start=True, stop=False)
            c.mm(pseg, ntri, o.dtAb[:, half * 4:(half + 1) * 4, :].re("p a b -> p (a b)"), start=False, stop=True)
            c.ts(o.segm[:, half * 4:(half + 1) * 4, :].re("p a b -> p (a b)"), pseg, 0.0, ALU.min)
        yield
        c.act(o.Dm, o.segm, AF.Exp)
        c.tt(o.MT, o.Dm, o.cbm.wap(o.cbm.ap.unsqueeze(1).broadcast_to([128, 8, 128])), ALU.mult)
        dtb = sm[:, 2, :].wap(sm.ap[:, 2, :].unsqueeze(2).broadcast_to([128, 8, 64]))
        c.tt(o.xdt, o.xs_tm, dtb, ALU.mult)
        dsb = sm[:, 4, :].wap(sm.ap[:, 4, :].unsqueeze(2).broadcast_to([128, 8, 64]))
        c.tt(o.xdd, o.xdt, dsb, ALU.mult, eng="pool")
        yield
        pyd = c.bank(6)
        for h in range(8):
            c.mm(pyd[:, h * 64:(h + 1) * 64], o.MT[:, h, :], o.xdt[:, h, :])
        yield
        pyo = c.bank(7)
        pH = c.bank(7)
        y1 = o.y1
        cd = sm[:, 5:7, :]
        c.mm(pyo, o.CTp[0], HTb[0], start=True, stop=True)
        eab = sm[:, 3, :].wap(sm.ap[:, 3, :].unsqueeze(2).broadcast_to([128, 8, 64]))
        c.tt(y1[0:64], pyo[0:64, :].re("p (h q) -> p h q", h=8), eab[0:64], ALU.mult)
        c.mm(pH, o.Bp[0], o.xdd.re("p h q -> p (h q)"))
        c.tt(HT, HT, cd[:, 0, :].wap(cd.ap[:, 0, :].unsqueeze(2).broadcast_to([128, 8, 64])), ALU.mult)
        c.tt(HT, HT, pH.re("p (h q) -> p h q", h=8), ALU.add)
        c.copy(HTb[1], HT.re("p h q -> p (h q)"), eng="act")
        yield
        c.mm(pyo, o.CTp[1], HTb[1], start=True, stop=True)
        c.tt(y1[64:128], pyo[64:128, :].re("p (h q) -> p h q", h=8), eab[64:128], ALU.mult)
        c.mm(pH, o.Bp[1], o.xdd.re("p h q -> p (h q)"))
        c.tt(HT, HT, cd[:, 1, :].wap(cd.ap[:, 1, :].unsqueeze(2).broadcast_to([128, 8, 64])), ALU.mult)
        c.tt(HT, HT, pH.re("p (h q) -> p h q", h=8), ALU.add)
        c.copy(HTb[0], HT.re("p h q -> p (h q)"), eng="act")
        yield
        y1f = y1.re("p h q -> p (h q)")
        sq, st = o.sq, o.st
        c.tt(y1f, pyd, y1f, ALU.add)
        dskb = spb[:, 2, :].wap(spb.ap[:, 2, :].unsqueeze(2).broadcast_to([128, 8, 64]))
        c.tt(sq.re("p (h q) -> p h q", h=8), o.xs_tm, dskb, ALU.mult, eng="pool")
        c.tt(y1f, y1f, sq, ALU.add)
        c.tt(y1f, y1f, o.sz, ALU.mult)
        yield
        c.tt(sq, y1f, y1f, ALU.mult, eng="pool")
        c.reduce(st[:, 0:1], sq)
        c.act(st[:, 1:2], st[:, 0:1], AF.Ln, scale=1.0 / 512, bias=K["eps"])
        c.act(st[:, 2:3], st[:, 1:2], AF.Exp, scale=-0.5)
        c.stt(o.yb, y1f, st[:, 2:3], nwb, ALU.mult, ALU.mult)
        yield
        for j in range(4):
            c.transpose(ptr[:, j, :], o.yb[:, j * 128:(j + 1) * 128], K["ident"])
        c.copy(ost[b % 2][:, :, tsl], ptr[:, 0:4, :], eng="act")
        if tt == nTT - 1:
            c.dma(out_dst(b * TB, TB).re("(cc p) t -> p cc t", p=128), ost[b % 2])
        yield

    starters = []
    gi = 0
    for b in range(nB):
        for tt in range(nTT):
            def start(b=b, tt=tt, gi=gi):
                if tt == 0:
                    prologue(b)
                return [tile_gen(b, tt, gi % 2)]
            starters.append(start)
            gi += 1
    run_pipeline(starters, MB_STAGGER, MB_MAXACT)
    c.release()


def build_p2_mamba(S):
    c = Ctx()
    hT = c.dram("hT", [D, S], BF16, "ExternalInput")
    wm = c.dram("wm", [D, 1288], F32, "ExternalInput")
    cw = c.dram("cw", [4, 768], F32, "ExternalInput")
    cb = c.dram("cb", [768], F32, "ExternalInput")
    sp = c.dram("sp", [3, 8], F32, "ExternalInput")
    nw = c.dram("snw", [512], F32, "ExternalInput")
    hc = c.dram("hc", [128, 512], F32, "ExternalInput")
    mc = c.dram("mc", [128, 512], F32, "ExternalInput")
    ident = c.dram("ident", [128, 128], F32, "ExternalInput")
    out = c.dram("ssT", [512, S], BF16, "ExternalOutput")
    K = setup_consts(c, ident)
    mamba_block(c, K, lambda t0, n: hT[:, t0:t0 + n], wm, cw, cb, sp, nw, hc, mc, lambda t0, n: out[:, t0:t0 + n], S)
    c.finish()
    c.emit()
    return c


def mamba_inputs(inp, l, g):
    w_in = inp["w_in"][l]
    xo = O_XBC
    wm = np.concatenate([w_in[:, O_Z + g * 512:O_Z + (g + 1) * 512], w_in[:, xo + g * 512:xo + (g + 1) * 512],
                         w_in[:, xo + 4096 + g * 128:xo + 4096 + (g + 1) * 128],
                         w_in[:, xo + 5120 + g * 128:xo + 5120 + (g + 1) * 128],
                         w_in[:, O_DT + g * 8:O_DT + (g + 1) * 8]], axis=1)
    sel = np.r_[g * 512:(g + 1) * 512, 4096 + g * 128:4096 + (g + 1) * 128, 5120 + g * 128:5120 + (g + 1) * 128]
    cw = inp["ssm_conv_w"][l][:, sel]
    cb = inp["ssm_conv_b"][l][sel]
    sp = np.stack([inp["ssm_a_log"][l][g * 8:(g + 1) * 8], inp["ssm_dt_bias"][l][g * 8:(g + 1) * 8],
                   inp["ssm_d"][l][g * 8:(g + 1) * 8]], axis=0)
    nw = inp["ssm_norm"][l][g * 512:(g + 1) * 512]
    return dict(wm=np.ascontiguousarray(wm), cw=np.ascontiguousarray(cw), cb=np.ascontiguousarray(cb),
                sp=np.ascontiguousarray(sp), snw=np.ascontiguousarray(nw))


def merge_block(c, K, x1, mix_src, x2, mix_nw, wg_d, woa_d, wob_d, woc_d, wout_d, TOK, cmap=None):
    TG = min(512, TOK)
    nH = TOK // TG
    nTT = TG // 128
    c.mark()
    hTh = c.sbuf("x_hTh", [128, KC, TG], BF16)
    mixh = c.sbuf("x_mix", [128, 64, TG], BF16)
    uT = c.sbuf("x_uT", [128, KC, TG], BF16)
    nwc = load_cols(c, "x_nwc", mix_nw, KC)
    comps = ((woa_d, 0, 16), (wob_d, 16, 16), (woc_d, 32, 32))
    for th in range(nH):
        t0 = th * TG
        c.dma(mixh, mix_src(t0, TG).re("(j p) t -> p j t", p=128))
        c.mark()
        norm_transpose(c, K, x1[t0:t0 + TG, :], nTT, hTh, nwc, tag="x_nt")
        c.release()
        c.mark()
        NB = 10
        wp = [c.sbuf("x_wp%d" % i, [128, 16, 256], BF16) for i in range(NB)]
        sg = [c.sbuf("x_sg%d" % i, [128, TG], F32) for i in range(2)]
        acc = [c.sbuf("x_acc%d" % i, [128, TG], F32) for i in range(2)]
        wi_ = [0]

        def nextbuf():
            b = wp[wi_[0] % NB]
            wi_[0] += 1
            return b

        it = 0
        for pr in range(D // 256):
            blks = []
            for ci, (wo_d, base, nch) in enumerate(comps):
                gb = nextbuf()
                c.dma(gb, wg_d[:, ci * D + pr * 256:ci * D + (pr + 1) * 256].re("(kc p) n -> p kc n", p=128), q="pool")
                wbs = []
                for part in range(nch // 16):
                    wb = nextbuf()
                    c.dma(wb, wo_d[part * 2048:(part + 1) * 2048, pr * 256:(pr + 1) * 256].re("(kc p) n -> p kc n", p=128),
                          q="pool")
                    wbs.append(wb)
                blks.append((gb, wbs, base, nch))
            for j in range(2):
                dcn = pr * 2 + j
                a = acc[dcn % 2]
                for ci, (gb, wbs, base, nch) in enumerate(blks):
                    pg = c.bank(it % 2, F32, [128, TG])
                    py = c.bank(2 + it % 2, F32, [128, TG])
                    s = sg[it % 2]
                    it += 1
                    for kc in range(KC):
                        c.mm(pg, gb[:, kc, j * 128:(j + 1) * 128], hTh[:, kc, :], start=(kc == 0), stop=(kc == KC - 1))
                    for ch in range(nch):
                        c.mm(py, wbs[ch // 16][:, ch % 16, j * 128:(j + 1) * 128],
                             mixh[:, (base + ch) if cmap is None else cmap(ci, ch), :],
                             start=(ch == 0), stop=(ch == nch - 1))
                    c.act(s, pg, AF.Sigmoid)
                    if ci == 0:
                        c.tt(a, py, s, ALU.mult)
                    elif ci == 1:
                        c.tt(s, py, s, ALU.mult)
                        c.tt(a, a, s, ALU.add, eng="pool")
                    else:
                        c.tt(s, py, s, ALU.mult)
                        c.tt(uT[:, dcn, :], a, s, ALU.add)
        c.release()
        c.mark()
        wob = [c.sbuf("x_wout%d" % i, [128, KC, 512], BF16) for i in range(2)]
        xr = [c.sbuf("x_xr%d" % i, [128, 512], F32) for i in range(4)]
        k = 0
        for dc in range(D // 512):
            wt = wob[dc % 2]
            c.dma(wt, wout_d[:, dc * 512:(dc + 1) * 512].re("(kc p) n -> p kc n", p=128), q="pool")
            for t in range(nTT):
                r = xr[k % 4]
                c.dma(r, x1[t0 + t * 128:t0 + (t + 1) * 128, dc * 512:(dc + 1) * 512])
                po = c.bank(4 + k % 4)
                k += 1
                for kc in range(KC):
                    c.mm(po, uT[:, kc, t * 128:(t + 1) * 128], wt[:, kc, :], start=(kc == 0), stop=(kc == KC - 1))
                c.tt(r, po, r, ALU.add)
                c.dma(x2[t0 + t * 128:t0 + (t + 1) * 128, dc * 512:(dc + 1) * 512], r)
        c.release()
    c.release()


def build_p3(TOK, last):
    c = Ctx()
    x1 = c.dram("x1", [TOK, D], F32, "ExternalInput")
    mixT = c.dram("mixT", [8192, TOK], BF16, "ExternalInput")
    mnw = c.dram("mix_nw", [D], F32, "ExternalInput")
    wg = c.dram("wg", [D, 3 * D], F32, "ExternalInput")
    woa = c.dram("woa", [2048, D], F32, "ExternalInput")
    wob = c.dram("wob", [2048, D], F32, "ExternalInput")
    woc = c.dram("woc", [4096, D], F32, "ExternalInput")
    wout = c.dram("wout", [D, D], F32, "ExternalInput")
    nw = c.dram("ffn_nw", [D], F32, "ExternalInput")
    wi = c.dram("ffn_wi", [D, 2 * DFF], F32, "ExternalInput")
    wo = c.dram("ffn_wo", [DFF, D], F32, "ExternalInput")
    ident = c.dram("ident", [128, 128], F32, "ExternalInput")
    x2 = c.dram("x2", [TOK, D], F32, "Internal")
    if last:
        fnw = c.dram("final_nw", [D], F32, "ExternalInput")
        x3 = c.dram("x3", [TOK, D], F32, "Internal")
        out = c.dram("out", [TOK, D], F32, "ExternalOutput")
    else:
        x3 = c.dram("out", [TOK, D], F32, "ExternalOutput")
    K = setup_consts(c, ident)
    merge_block(c, K, x1, lambda t0, n: mixT[:, t0:t0 + n], x2, mnw, wg, woa, wob, woc, wout, TOK)
    ffn_block(c, K, x2, x3, nw, wi, wo, TOK)
    if last:
        final_norm_block(c, K, x3, out, fnw, TOK)
    c.finish()
    c.emit()
    return c


def build_p2(S, layer):
    c = Ctx()
    hT = c.dram("hT", [D, S], BF16, "ExternalInput")
    latT = c.dram("latT", [1088, S], BF16, "ExternalInput")
    wq = c.dram("wq", [512, 384], F32, "ExternalInput")
    wkv = c.dram("wkv", [512, 512], F32, "ExternalInput")
    rope = c.dram("rope_cs", [S, 64], F32, "ExternalInput")
    wh = c.dram("wh", [D, 1024], F32, "ExternalInput")
    lg = c.dram("lg", [2, 256], F32, "ExternalInput")
    hnw = c.dram("hnw", [256], F32, "ExternalInput")
    wm = c.dram("wm", [D, 1288], F32, "ExternalInput")
    cw = c.dram("cw", [4, 768], F32, "ExternalInput")
    cb = c.dram("cb", [768], F32, "ExternalInput")
    sp = c.dram("sp", [3, 8], F32, "ExternalInput")
    snw = c.dram("snw", [512], F32, "ExternalInput")
    hc = c.dram("hc", [128, 512], F32, "ExternalInput")
    mc = c.dram("mc", [128, 512], F32, "ExternalInput")
    ident = c.dram("ident", [128, 128], F32, "ExternalInput")
    out = c.dram("mixT", [1024, S], BF16, "ExternalOutput")
    K = setup_consts(c, ident)
    mla_block(c, K, lambda t0, n: latT[0:1024, t0:t0 + n], lambda t0, n: latT[1024:1088, t0:t0 + n], wq, wkv, rope,
              lambda h, t0, n: out[h * 128:(h + 1) * 128, t0:t0 + n], S)
    hgrn_block(c, K, lambda t0, n: hT[:, t0:t0 + n], wh, lg, hnw, hc,
               lambda h, t0, n: out[256 + h * 128:256 + (h + 1) * 128, t0:t0 + n], S, layer)
    mamba_block(c, K, lambda t0, n: hT[:, t0:t0 + n], wm, cw, cb, sp, snw, hc, mc,
                lambda t0, n: out[512:1024, t0:t0 + n], S)
    c.finish()
    c.emit()
    return c


_CACHE = {}


def _get(key, fn):
    if key not in _CACHE:
        _CACHE[key] = fn()
    return _CACHE[key]


def build_tok(TOK, has_merge, has_p1, last):
    c = Ctx()
    ident = c.dram("ident", [128, 128], F32, "ExternalInput")
    K = setup_consts(c, ident)
    xin = c.dram("xin", [TOK, D], F32, "ExternalInput")
    cur = xin
    if has_merge:
        mixT = c.dram("mixT", [8192, TOK], BF16, "ExternalInput")
        mnw = c.dram("m_mix_nw", [D], F32, "ExternalInput")
        wg = c.dram("wg", [D, 3 * D], F32, "ExternalInput")
        woa = c.dram("woa", [2048, D], F32, "ExternalInput")
        wob = c.dram("wob", [2048, D], F32, "ExternalInput")
        woc = c.dram("woc", [4096, D], F32, "ExternalInput")
        wout = c.dram("wout", [D, D], F32, "ExternalInput")
        nw2 = c.dram("ffn2_nw", [D], F32, "ExternalInput")
        wi2 = c.dram("ffn2_wi", [D, 2 * DFF], F32, "ExternalInput")
        wo2 = c.dram("ffn2_wo", [DFF, D], F32, "ExternalInput")
        x2 = c.dram("x2", [TOK, D], F32, "Internal")
        if last:
            fnw = c.dram("final_nw", [D], F32, "ExternalInput")
            x3 = c.dram("x3", [TOK, D], F32, "Internal")
            out = c.dram("out", [TOK, D], F32, "ExternalOutput")
        else:
            x3 = c.dram("x3", [TOK, D], F32, "Internal")
        merge_block(c, K, cur, lambda t0, n: mixT[:, t0:t0 + n], x2, mnw, wg, woa, wob, woc, wout, TOK)
        ffn_block(c, K, x2, x3, nw2, wi2, wo2, TOK)
        cur = x3
        if last:
            final_norm_block(c, K, x3, out, fnw, TOK)
    if has_p1:
        nw = c.dram("ffn1_nw", [D], F32, "ExternalInput")
        wi = c.dram("ffn1_wi", [D, 2 * DFF], F32, "ExternalInput")
        wo = c.dram("ffn1_wo", [DFF, D], F32, "ExternalInput")
        mnw1 = c.dram("mix_nw", [D], F32, "ExternalInput")
        wlat = c.dram("w_lat", [D, 1088], F32, "ExternalInput")
        qn = c.dram("qn_w", [512], F32, "ExternalInput")
        kvn = c.dram("kvn_w", [512], F32, "ExternalInput")
        rope = c.dram("rope_cs", [TOK, 64], F32, "ExternalInput")
        x1 = c.dram("x1", [TOK, D], F32, "ExternalOutput")
        hT = c.dram("hT", [D, TOK], BF16, "ExternalOutput")
        latT = c.dram("latT", [1088, TOK], BF16, "ExternalOutput")
        ffn_block(c, K, cur, x1, nw, wi, wo, TOK)
        latents_block(c, K, x1, hT, latT, mnw1, wlat, qn, kvn, rope, TOK)
    c.finish()
    c.emit()
    return c


def kernel(x, ffn1_norm, ffn1_wi, ffn1_wo, mix_norm, w_in, mla_q_norm, mla_w_uq, mla_kv_norm, mla_w_ukv,
           hgrn_lb_logits, hgrn_norm, ssm_conv_w, ssm_conv_b, ssm_a_log, ssm_dt_bias, ssm_d, ssm_norm,
           w_o_mla, w_o_hgrn, w_o_ssm, w_out, ffn2_norm, ffn2_wi, ffn2_wo, final_norm):
    inp = dict(w_in=w_in, ssm_conv_w=ssm_conv_w, ssm_conv_b=ssm_conv_b, ssm_a_log=ssm_a_log,
               ssm_dt_bias=ssm_dt_bias, ssm_d=ssm_d, ssm_norm=ssm_norm)
    inp = {k: np.asarray(v, dtype=np.float32) for k, v in inp.items()}
    f32 = lambda a: np.ascontiguousarray(np.asarray(a, dtype=np.float32))
    S = x.shape[1]
    TOK = S // NCORES
    L = 2
    cores = list(range(NCORES))
    ident = np.eye(128, dtype=np.float32)
    rope = rope_table(S)
    hcs, mcs = hgrn_consts(), mamba_consts()
    xs = [f32(x[0, i * TOK:(i + 1) * TOK]) for i in cores]
    mixT_full = None
    x1 = None
    for l in range(L + 1):
        has_merge = l > 0
        has_p1 = l < L
        last = (l == L)
        ct = _get(("tok", TOK, has_merge, has_p1, last), lambda: build_tok(TOK, has_merge, has_p1, last))
        maps = []
        for i in cores:
            d = dict(ident=ident)
            if has_merge:
                lm = l - 1
                d.update(xin=x1[i], mixT=np.ascontiguousarray(mixT_full[:, i * TOK:(i + 1) * TOK]),
                         m_mix_nw=f32(mix_norm[lm]), wg=f32(w_in[lm][:, O_GATE:]), woa=f32(w_o_mla[lm]),
                         wob=f32(w_o_hgrn[lm]), woc=f32(w_o_ssm[lm]), wout=f32(w_out[lm]),
                         ffn2_nw=f32(ffn2_norm[lm]), ffn2_wi=f32(ffn2_wi[lm]), ffn2_wo=f32(ffn2_wo[lm]))
                if last:
                    d["final_nw"] = f32(final_norm)
            else:
                d["xin"] = xs[i]
            if has_p1:
                d.update(ffn1_nw=f32(ffn1_norm[l]), ffn1_wi=f32(ffn1_wi[l]), ffn1_wo=f32(ffn1_wo[l]),
                         mix_nw=f32(mix_norm[l]), w_lat=f32(w_in[l][:, :1088]), qn_w=f32(mla_q_norm[l]),
                         kvn_w=f32(mla_kv_norm[l]), rope_cs=rope[i * TOK:(i + 1) * TOK])
            maps.append(d)
        r1 = run_bass_kernel_spmd(ct.nc, maps, core_ids=cores).results
        if last:
            return np.concatenate([np.asarray(r["out"]) for r in r1], axis=0)[None].astype(np.float32)
        x1 = [np.asarray(r["x1"]) for r in r1]
        hT_all = np.ascontiguousarray(np.concatenate([np.asarray(r["hT"]) for r in r1], axis=1))
        latT_all = np.ascontiguousarray(np.concatenate([np.asarray(r["latT"]) for r in r1], axis=1))
        c2 = _get(("p2", S, l), lambda: build_p2(S, l))
        maps = []
        for i in cores:
            wq, wkv = mla_weights(f32(mla_w_uq[l]), f32(mla_w_ukv[l]), i)
            d = mamba_inputs(inp, l, i)
            d.update(hT=hT_all, latT=latT_all, wq=wq, wkv=wkv, rope_cs=rope, wh=hgrn_weights(inp["w_in"][l], i),
                     lg=f32(np.asarray(hgrn_lb_logits)[:, i * 256:(i + 1) * 256]),
                     hnw=f32(np.asarray(hgrn_norm)[l][i * 256:(i + 1) * 256]), hc=hcs, mc=mcs, ident=ident)
            maps.append(d)
        r2 = run_bass_kernel_spmd(c2.nc, maps, core_ids=cores).results
        mix = [np.asarray(r["mixT"]) for r in r2]
        mixT_full = np.concatenate([m[0:256] for m in mix] + [m[256:512] for m in mix] + [m[512:1024] for m in mix], axis=0)
```
